# Optimizing a Trainium2 kernel written in Bass

```python
import jax, jax.numpy as jnp
from jax import lax
import numpy as np

D_MODEL = 1024
BATCH = 2
SEQ = 8192
DEPTH = 4

NSA_HEADS = 8
NSA_KV_HEADS = 2
NSA_HEAD_DIM = 64
NSA_GROUP = NSA_HEADS // NSA_KV_HEADS
NSA_WIDTH = NSA_HEADS * NSA_HEAD_DIM
CMP_BLOCK = 32
CMP_STRIDE = 16
SLC_BLOCK = 64
SLC_TOPK = 16
WINDOW = 512
Q_BLOCK = 128
FORCE_SCORE = 1e4
NEG_INF = -1e30
RWKV_HEADS = 8
RWKV_HEAD_DIM = 64
RWKV_WIDTH = RWKV_HEADS * RWKV_HEAD_DIM
DECAY_RANK = 64
ICLR_RANK = 64
VRES_RANK = 32
GATE_RANK = 128
LNX_EPS = 1e-5 * RWKV_HEAD_DIM
D_FF = 2816
CONV_WIDTH = 3
EPS = 1e-6
NSA_Q_COLS = NSA_WIDTH
NSA_KV_COLS = NSA_KV_HEADS * NSA_HEAD_DIM
NSA_GATE_COLS = NSA_HEADS * 3
RWKV_SPLITS = (RWKV_WIDTH, RWKV_WIDTH, RWKV_WIDTH, DECAY_RANK, ICLR_RANK, GATE_RANK)
RWKV_COLS = sum(RWKV_SPLITS)
MERGE_COLS = 2 * D_MODEL
IN_SPLITS = (NSA_Q_COLS, NSA_KV_COLS, NSA_KV_COLS, NSA_KV_COLS, NSA_KV_COLS, NSA_KV_COLS, NSA_KV_COLS, NSA_GATE_COLS, RWKV_COLS, MERGE_COLS)
IN_COLS = sum(IN_SPLITS)

kernel_name = 'nsa_rwkv7_gated_hybrid'


def _split(z, sizes):
    return jnp.split(z, np.cumsum(sizes)[:-1].tolist(), axis=-1)


def _rms_norm(x, g):
    x32 = x.astype(jnp.float32)
    y = x32 * lax.rsqrt(jnp.mean(x32 * x32, axis=-1, keepdims=True) + EPS)
    return (y * g.astype(jnp.float32)).astype(x.dtype)


def _masked_softmax(s, mask):
    p = jax.nn.softmax(jnp.where(mask, s.astype(jnp.float32), NEG_INF), axis=-1) * mask
    return p.astype(s.dtype)


def _nsa(q, kc, vc, ks, vs, kw, vw, gate_logits, qk_gain, cmp_pe, cmp_w1, cmp_w2):
    B, S, _ = q.shape
    dt = q.dtype
    G, HG, Dh = NSA_KV_HEADS, NSA_GROUP, NSA_HEAD_DIM
    scale = Dh ** -0.5
    qh = _rms_norm(q.reshape(B, S, G, HG, Dh), qk_gain[0]).transpose(0, 2, 3, 1, 4)

    nc = (S - CMP_BLOCK) // CMP_STRIDE + 1
    tok = np.arange(nc)[:, None] * CMP_STRIDE + np.arange(CMP_BLOCK)[None, :]

    def compress(z, pe, w1, w2):
        blocks = z.reshape(B, S, G, Dh)[:, tok] + pe[None, None, :, None, :]
        hid = jax.nn.silu(jnp.einsum('bnlgd,lde->bnge', blocks, w1))
        return jnp.einsum('bnge,ef->bgnf', hid, w2)

    k_cmp = _rms_norm(compress(kc, cmp_pe[0], cmp_w1[0], cmp_w2[0]), qk_gain[1])
    v_cmp = compress(vc, cmp_pe[1], cmp_w1[1], cmp_w2[1])
    cmp_end = jnp.arange(nc) * CMP_STRIDE + CMP_BLOCK - 1

    ns = S // SLC_BLOCK
    n_sel = min(SLC_TOPK, ns)
    k_slc = _rms_norm(ks.reshape(B, S, G, Dh), qk_gain[2]).transpose(0, 2, 1, 3).reshape(B, G, ns, SLC_BLOCK, Dh)
    v_slc = vs.reshape(B, S, G, Dh).transpose(0, 2, 1, 3).reshape(B, G, ns, SLC_BLOCK, Dh)
    n_idx = jnp.arange(nc)[:, None]
    s_idx = jnp.arange(ns)
    overlap = ((n_idx * CMP_STRIDE < (s_idx[None, :] + 1) * SLC_BLOCK)
               & (n_idx * CMP_STRIDE + CMP_BLOCK > s_idx[None, :] * SLC_BLOCK)).astype(jnp.float32)

    pad = ((0, 0), (0, 0), (WINDOW, 0), (0, 0))
    k_win = jnp.pad(_rms_norm(kw.reshape(B, S, G, Dh), qk_gain[3]).transpose(0, 2, 1, 3), pad)
    v_win = jnp.pad(vw.reshape(B, S, G, Dh).transpose(0, 2, 1, 3), pad)

    gates = jax.nn.sigmoid(gate_logits.astype(jnp.float32)).astype(dt)
    gates = gates.reshape(B, S, G, HG, 3).transpose(0, 2, 3, 1, 4)
    bi = jnp.arange(B)[:, None, None, None]
    gi = jnp.arange(G)[None, :, None, None]

    def block(c):
        t0 = c * Q_BLOCK
        t = t0 + jnp.arange(Q_BLOCK)
        qb = lax.dynamic_slice_in_dim(qh, t0, Q_BLOCK, axis=3)
        gb = lax.dynamic_slice_in_dim(gates, t0, Q_BLOCK, axis=3)
        s = jnp.einsum('bghtd,bgnd->bghtn', qb, k_cmp) * scale
        p_cmp = _masked_softmax(s, cmp_end[None, :] <= t[:, None])
        o_cmp = jnp.einsum('bghtn,bgnd->bghtd', p_cmp, v_cmp)
        imp = jnp.einsum('bghtn,ns->bgts', p_cmp.astype(jnp.float32), overlap)
        cur = t // SLC_BLOCK
        forced = (s_idx[None, :] == 0) | (s_idx[None, :] == cur[:, None]) | (s_idx[None, :] == cur[:, None] - 1)
        imp = jnp.where(forced, FORCE_SCORE, imp)
        imp = jnp.where(s_idx[None, :] > cur[:, None], -1.0, imp)
        _, sel = lax.top_k(imp, n_sel)
        k_sel = k_slc[bi, gi, sel].reshape(B, G, Q_BLOCK, n_sel * SLC_BLOCK, Dh)
        v_sel = v_slc[bi, gi, sel].reshape(B, G, Q_BLOCK, n_sel * SLC_BLOCK, Dh)
        pos = (sel[..., None] * SLC_BLOCK + jnp.arange(SLC_BLOCK)).reshape(B, G, Q_BLOCK, n_sel * SLC_BLOCK)
        s = jnp.einsum('bghtd,bgtkd->bghtk', qb, k_sel) * scale
        p_slc = _masked_softmax(s, pos[:, :, None] <= t[:, None])
        o_slc = jnp.einsum('bghtk,bgtkd->bghtd', p_slc, v_sel)
        k_w = lax.dynamic_slice_in_dim(k_win, t0, WINDOW + Q_BLOCK, axis=2)
        v_w = lax.dynamic_slice_in_dim(v_win, t0, WINDOW + Q_BLOCK, axis=2)
        kpos = t0 - WINDOW + jnp.arange(WINDOW + Q_BLOCK)
        dist = t[:, None] - kpos[None, :]
        s = jnp.einsum('bghtd,bgkd->bghtk', qb, k_w) * scale
        p_win = _masked_softmax(s, (kpos[None, :] >= 0) & (dist >= 0) & (dist < WINDOW))
        o_win = jnp.einsum('bghtk,bgkd->bghtd', p_win, v_w)
        return gb[..., 0:1] * o_cmp + gb[..., 1:2] * o_slc + gb[..., 2:3] * o_win

    o = lax.map(block, jnp.arange(S // Q_BLOCK))
    return o.transpose(1, 0, 4, 2, 3, 5).reshape(B, S, NSA_WIDTH)


def _wkv7_scan(r, w, k, v, a, b):
    B, S, H, N = r.shape

    def step(state, inp):
        r_t, w_t, k_t, v_t, a_t, b_t = inp
        sa = jnp.einsum('bhvk,bhk->bhv', state, a_t)
        state = state * w_t[:, :, None, :] + sa[..., None] * b_t[:, :, None, :] + v_t[..., None] * k_t[:, :, None, :]
        return state, jnp.einsum('bhvk,bhk->bhv', state, r_t)

    xs = (jnp.swapaxes(r, 0, 1), jnp.swapaxes(w, 0, 1), jnp.swapaxes(k, 0, 1),
          jnp.swapaxes(v, 0, 1), jnp.swapaxes(a, 0, 1), jnp.swapaxes(b, 0, 1))
    _, y = lax.scan(step, jnp.zeros((B, H, N, N), jnp.float32), xs)
    return jnp.swapaxes(y, 0, 1)


def _rwkv7(z, v_first, vres, mu, w0, w_up, a0, a_up, g_up, k_k, k_a, r_k, ln_w, ln_b):
    B, S, _ = z.shape
    dt = z.dtype
    f32 = jnp.float32
    H, N = RWKV_HEADS, RWKV_HEAD_DIM
    z_prev = jnp.pad(z[:, :-1], ((0, 0), (1, 0), (0, 0)))
    z = z + mu * (z_prev - z)
    r, k, v, wd, ad, gd = _split(z, RWKV_SPLITS)
    if vres is None:
        v_first = v
    else:
        v0, v1, v2 = vres
        v = v + (v_first - v) * jax.nn.sigmoid(v0 + (v @ v1) @ v2)
    w = -jax.nn.softplus(-(w0 + jnp.tanh(wd) @ w_up).astype(f32)) - 0.5
    decay = jnp.exp(-jnp.exp(w))
    a = jax.nn.sigmoid((a0 + ad @ a_up).astype(f32))
    g = jax.nn.sigmoid(gd) @ g_up

    def heads(t):
        return t.astype(f32).reshape(B, S, H, N)

    rh, vh, ah, wh = heads(r), heads(v), heads(a), heads(decay)
    kk = heads(k * k_k)
    kk = kk / jnp.maximum(jnp.sqrt(jnp.sum(kk * kk, axis=-1, keepdims=True)), 1e-12)
    kh = heads(k) * (1.0 + (ah - 1.0) * k_a.astype(f32).reshape(H, N))
    y = _wkv7_scan(rh, wh, kh, vh, -kk, kk * ah)
    mean = jnp.mean(y, axis=-1, keepdims=True)
    var = jnp.mean(jnp.square(y - mean), axis=-1, keepdims=True)
    y = (y - mean) * lax.rsqrt(var + LNX_EPS) * ln_w.astype(f32).reshape(H, N) + ln_b.astype(f32).reshape(H, N)
    y = y + jnp.sum(rh * kh * r_k.astype(f32), axis=-1, keepdims=True) * vh
    return y.reshape(B, S, RWKV_WIDTH).astype(dt) * g, v_first


def _conv_ffn(h, w_up, conv_w, w_down):
    u = h @ w_up
    u = lax.conv_general_dilated(u, conv_w[:, None, :], window_strides=(1,),
                                 padding=((CONV_WIDTH - 1, 0),),
                                 dimension_numbers=('NWC', 'WIO', 'NWC'),
                                 feature_group_count=u.shape[-1])
    a, b = jnp.split(u, 2, axis=-1)
    return (jax.nn.silu(a) * b) @ w_down


def setup_inputs(seed: int = 0) -> dict:
    key = jax.random.key(seed)
    k = jax.random.split(key, 32)
    L, D, Dh = DEPTH, D_MODEL, NSA_HEAD_DIM
    f32 = jnp.float32

    def nrm(kk, shape, scale):
        return jax.random.normal(kk, shape, f32) * scale

    def gain(kk, shape):
        return 1.0 + 0.1 * jax.random.normal(kk, shape, f32)

    return {
        'x': nrm(k[0], (BATCH, SEQ, D), 1.0),
        'norm_mix': gain(k[1], (L, D)),
        'norm_ffn': gain(k[2], (L, D)),
        'w_in': nrm(k[3], (L, D, IN_COLS), D ** -0.5),
        'qk_gain': gain(k[4], (L, 4, Dh)),
        'cmp_pe': nrm(k[5], (L, 2, CMP_BLOCK, Dh), 0.1),
        'cmp_w1': nrm(k[6], (L, 2, CMP_BLOCK, Dh, Dh), (CMP_BLOCK * Dh) ** -0.5),
        'cmp_w2': nrm(k[7], (L, 2, Dh, Dh), Dh ** -0.5),
        'rwkv_mu': jax.random.uniform(k[8], (L, RWKV_COLS), f32),
        'rwkv_w0': jax.random.uniform(k[9], (L, RWKV_WIDTH), f32, -3.0, 2.0),
        'rwkv_w_up': nrm(k[10], (L, DECAY_RANK, RWKV_WIDTH), 0.1),
        'rwkv_a0': nrm(k[11], (L, RWKV_WIDTH), 0.5),
        'rwkv_a_up': nrm(k[12], (L, ICLR_RANK, RWKV_WIDTH), ICLR_RANK ** -0.5),
        'rwkv_g_up': nrm(k[13], (L, GATE_RANK, RWKV_WIDTH), GATE_RANK ** -0.5),
        'rwkv_k_k': 0.85 + 0.1 * jax.random.normal(k[14], (L, RWKV_WIDTH), f32),
        'rwkv_k_a': gain(k[15], (L, RWKV_WIDTH)),
        'rwkv_r_k': nrm(k[16], (L, RWKV_HEADS, RWKV_HEAD_DIM), 0.1),
        'rwkv_ln_w': gain(k[17], (L, RWKV_WIDTH)),
        'rwkv_ln_b': nrm(k[18], (L, RWKV_WIDTH), 0.02),
        'vres_v0': nrm(k[19], (L - 1, RWKV_WIDTH), 0.5),
        'vres_v1': nrm(k[20], (L - 1, RWKV_WIDTH, VRES_RANK), RWKV_WIDTH ** -0.5),
        'vres_v2': nrm(k[21], (L - 1, VRES_RANK, RWKV_WIDTH), VRES_RANK ** -0.5),
        'proj_nsa': nrm(k[22], (L, NSA_WIDTH, D), NSA_WIDTH ** -0.5),
        'proj_rwkv': nrm(k[23], (L, RWKV_WIDTH, D), RWKV_WIDTH ** -0.5),
        'w_out': nrm(k[24], (L, D, D), D ** -0.5),
        'ffn_up': nrm(k[25], (L, D, 2 * D_FF), D ** -0.5),
        'ffn_conv': nrm(k[26], (L, CONV_WIDTH, 2 * D_FF), CONV_WIDTH ** -0.5),
        'ffn_down': nrm(k[27], (L, D_FF, D), D_FF ** -0.5),
    }


def reference(x, norm_mix, norm_ffn, w_in, qk_gain, cmp_pe, cmp_w1, cmp_w2, rwkv_mu, rwkv_w0,
              rwkv_w_up, rwkv_a0, rwkv_a_up, rwkv_g_up, rwkv_k_k, rwkv_k_a, rwkv_r_k, rwkv_ln_w,
              rwkv_ln_b, vres_v0, vres_v1, vres_v2, proj_nsa, proj_rwkv, w_out, ffn_up, ffn_conv,
              ffn_down):
    B, S, D = x.shape
    v_first = None
    for l in range(DEPTH):
        h = _rms_norm(x, norm_mix[l])
        u = h @ w_in[l]
        q, kc, vc, ks, vs, kw, vw, nsa_gl, rw, merge = _split(u, IN_SPLITS)
        o_nsa = _nsa(q, kc, vc, ks, vs, kw, vw, nsa_gl, qk_gain[l], cmp_pe[l], cmp_w1[l], cmp_w2[l])
        vres = None if l == 0 else (vres_v0[l - 1], vres_v1[l - 1], vres_v2[l - 1])
        o_rwkv, v_first = _rwkv7(rw, v_first, vres, rwkv_mu[l], rwkv_w0[l], rwkv_w_up[l], rwkv_a0[l],
                                 rwkv_a_up[l], rwkv_g_up[l], rwkv_k_k[l], rwkv_k_a[l], rwkv_r_k[l],
                                 rwkv_ln_w[l], rwkv_ln_b[l])
        gates = jax.nn.sigmoid(merge.astype(jnp.float32)).astype(x.dtype).reshape(B, S, 2, D)
        y = gates[:, :, 0] * (o_nsa @ proj_nsa[l]) + gates[:, :, 1] * (o_rwkv @ proj_rwkv[l])
        x = x + y @ w_out[l]
        x = x + _conv_ffn(_rms_norm(x, norm_ffn[l]), ffn_up[l], ffn_conv[l], ffn_down[l])
    return x
```

```python
import contextlib
import math
import numpy as np
import concourse.bass as bass
import concourse.mybir as mybir
from concourse.bass_utils import run_bass_kernel_spmd

F32 = mybir.dt.float32
BF16 = mybir.dt.bfloat16
AF = mybir.ActivationFunctionType
ALU = mybir.AluOpType
AX = mybir.AxisListType


class Prog:
    COMPUTE = ('pe', 'act', 'dve', 'pool')
    NSLOT = 6

    def __init__(self, nc, same_engine_sync=True):
        self.nc = nc
        self.streams = {e: [] for e in ('pe', 'act', 'dve', 'pool', 'sp')}
        self.count = {e: 0 for e in self.streams}
        self.waited = {e: {} for e in self.streams}
        self.last_w = {}
        self.readers = {}
        self.dma_slot = {q: 0 for q in ('sp', 'act', 'pool')}
        self.dma_cnt = {}
        self.same_engine_sync = same_engine_sync
        self.out_dma = []

    def _need(self, eng, dep):
        semk, val = dep
        if semk == eng and (eng == 'pe' or not self.same_engine_sync):
            return
        w = self.waited[eng]
        if w.get(semk, 0) >= val:
            return
        w[semk] = val
        self.streams[eng].append(('wait', semk, val))

    @staticmethod
    def _k(k):
        if isinstance(k, (str, tuple, int)):
            return k
        return 'T:' + str(k.name)

    def _deps(self, eng, r, w):
        r = [self._k(k) for k in r]
        w = [self._k(k) for k in w]
        for k in r:
            if k in self.last_w:
                self._need(eng, self.last_w[k])
        for k in w:
            if k in self.last_w:
                self._need(eng, self.last_w[k])
            for d in self.readers.get(k, ()):
                self._need(eng, d)

    def _record(self, tag, r, w):
        r = [self._k(k) for k in r]
        w = [self._k(k) for k in w]
        for k in r:
            self.readers.setdefault(k, []).append(tag)
        for k in w:
            self.last_w[k] = tag
            self.readers[k] = []

    def op(self, eng, fn, r=(), w=()):
        self._deps(eng, r, w)
        self.count[eng] += 1
        tag = (eng, self.count[eng])
        self.streams[eng].append(('inst', fn, eng, 1))
        self._record(tag, r, w)

    def i(self, eng, name, r=(), w=(), **kw):
        self.op(eng, lambda e: getattr(e, name)(**kw), r=r, w=w)

    def d(self, q, r=(), w=(), final=False, **kw):
        self.dma(q, lambda e: e.dma_start(**kw), r=r, w=w, final=final)

    def dma(self, q, fn, r=(), w=(), final=False):
        s = self.dma_slot[q]
        self.dma_slot[q] = (s + 1) % self.NSLOT
        semk = ('dma', q, s)
        prev = self.dma_cnt.get(semk, 0)
        if prev:
            self._need(q, (semk, prev))
        self._deps(q, r, w)
        self.dma_cnt[semk] = prev + 16
        tag = (semk, prev + 16)
        self.streams[q].append(('inst', fn, semk, 16))
        self._record(tag, r, w)
        if final:
            self.out_dma.append(tag)

    def emit(self):
        nc = self.nc
        semkeys = list(self.COMPUTE) + [('dma', q, s) for q in ('sp', 'act', 'pool') for s in range(self.NSLOT)]
        for tag in self.out_dma:
            self._need('sp', tag)
        import contextlib
        with contextlib.ExitStack() as st:
            sems = {}
            for i, k in enumerate(semkeys):
                sems[k] = st.enter_context(nc.semaphore("s%d" % i))
            block = st.enter_context(nc.Block())

            def replay(name):
                def f(e):
                    for it in self.streams[name]:
                        if it[0] == 'wait':
                            e.wait_ge(sems[it[1]], it[2])
                        else:
                            it[1](e).then_inc(sems[it[2]], it[3])
                return f
            block.sync(replay('sp'))
            block.scalar(replay('act'))
            block.vector(replay('dve'))
            block.gpsimd(replay('pool'))
            block.tensor(replay('pe'))


EPS = 1e-6
D = 1024

class Ctx:
    def __init__(self):
        self.nc = bass.Bass("TRN2", target_bir_lowering=False)
        self.P = Prog(self.nc)
        self.st = contextlib.ExitStack()
        self.n = 0
        self.psn = 0
        self.rr = 0
    def dram_in(self, name, shape, dt=F32):
        return self.nc.dram_tensor(name, list(shape), dt, kind="ExternalInput").ap()
    def dram_out(self, name, shape, dt=F32):
        return self.nc.dram_tensor(name, list(shape), dt, kind="ExternalOutput").ap()
    def dram_tmp(self, name, shape, dt=F32):
        return self.nc.dram_tensor(name, list(shape), dt, kind="Internal").ap()
    def sb(self, name, shape, dt=F32):
        return self.st.enter_context(self.nc.sbuf_tensor(name, list(shape), dt))
    def ps(self, name, shape, dt=F32):
        return self.st.enter_context(self.nc.psum_tensor(name, list(shape), dt))
    def psbanks(self, n=8):
        self.banks = [self.ps("psb%d" % i, [128, 512]) for i in range(n)]
        return self.banks
    def bank(self):
        b = self.banks[self.psn % len(self.banks)]
        self.psn += 1
        return b
    def finish(self):
        self.P.emit()
        self.st.close()
        return self.nc

    def load_w_bf16(self, name, w_ap, K, N, SC=2048):
        P = self.P
        KC = K // 128
        wb = self.sb(name, [128, KC, N], BF16)
        if not hasattr(self, 'stage'):
            self.stage = [self.sb("wstage%d" % i, [128, SC]) for i in range(2)]
        i = self.rr
        for c in range(KC):
            for n0 in range(0, N, SC):
                n1 = min(N, n0 + SC)
                sgt = self.stage[i % 2]
                P.dma('sp', (lambda sgt, c, n0, n1: lambda e: e.dma_start(out=sgt[:, 0:n1 - n0], in_=w_ap[c * 128:(c + 1) * 128, n0:n1]))(sgt, c, n0, n1), w=[sgt])
                eng = 'dve' if i % 2 == 0 else 'pool'
                P.op(eng, (lambda sgt, c, n0, n1: lambda e: e.tensor_copy(out=wb[:, c, n0:n1], in_=sgt[:, 0:n1 - n0]))(sgt, c, n0, n1), r=[sgt], w=[wb])
                i += 1
        self.rr = i
        return wb

    def load_vec_t(self, name, v_ap, n):
        t = self.sb(name, [128, n // 128])
        self.P.dma('sp', lambda e: e.dma_start(out=t[:], in_=v_ap.rearrange("(c p) -> p c", p=128), allow_slow_non_contiguous=True), w=[t])
        return t

    def consts(self):
        P = self.P
        self.ones_bf = self.sb("ones_bf", [128, 128], BF16)
        P.op('pool', lambda e: e.memset(self.ones_bf[:], 1.0), w=[self.ones_bf])

    def rmsnorm(self, xt, gain_t, w, hT, sq, rstd):
        P = self.P
        for c in range(8):
            P.op('act', (lambda c: lambda e: e.activation(out=sq[:, c, 0:w], in_=xt[:, c, 0:w], func=AF.Square))(c), r=[xt], w=[sq])
        pb = self.bank()
        for c in range(8):
            P.op('pe', (lambda c: lambda e: e.matmul(pb[:, 0:w], lhsT=self.ones_bf[:], rhs=sq[:, c, 0:w], start=(c == 0), stop=(c == 7)))(c), r=[sq, self.ones_bf], w=[pb])
        P.op('dve', lambda e: e.tensor_scalar(out=rstd[:, 0:w], in0=pb[:, 0:w], scalar1=1.0 / D, scalar2=EPS, op0=ALU.mult, op1=ALU.add), r=[pb], w=[rstd])
        P.op('act', lambda e: e.activation(out=rstd[:, 0:w], in_=rstd[:, 0:w], func=AF.Sqrt), r=[rstd], w=[rstd])
        P.op('dve', lambda e: e.reciprocal(out=rstd[:, 0:w], in_=rstd[:, 0:w]), r=[rstd], w=[rstd])
        for c in range(8):
            P.op('dve', (lambda c: lambda e: e.scalar_tensor_tensor(out=hT[:, c, 0:w], in0=xt[:, c, 0:w], scalar=gain_t[:, c:c + 1], in1=rstd[:, 0:w], op0=ALU.mult, op1=ALU.mult))(c), r=[xt, gain_t, rstd], w=[hT])


def build_proj(T, NCOL):
    C = Ctx(); P = C.P
    xT = C.dram_in("xT", [D, T]); gain = C.dram_in("gain", [D]); W = C.dram_in("W", [D, NCOL])
    uT = C.dram_out("uT", [NCOL, T])
    C.consts(); C.psbanks()
    gain_t = C.load_vec_t("gain_t", gain, D)
    Wb = C.load_w_bf16("Wb", W, D, NCOL)
    TW = 512
    xts = [C.sb("xt%d" % i, [128, 8, TW]) for i in range(2)]
    sq = C.sb("sq", [128, 8, TW], BF16); rstd = C.sb("rstd", [128, TW])
    hTs = [C.sb("hT%d" % i, [128, 8, TW], BF16) for i in range(2)]
    obs = [C.sb("ob%d" % i, [128, TW]) for i in range(4)]
    xv = xT.rearrange("(c p) t -> p c t", p=128)
    oi = 0
    for ti, t0 in enumerate(range(0, T, TW)):
        xt = xts[ti % 2]; hT = hTs[ti % 2]
        P.dma('sp', (lambda xt, t0: lambda e: e.dma_start(out=xt[:], in_=xv[:, :, t0:t0 + TW]))(xt, t0), w=[xt])
        C.rmsnorm(xt, gain_t, TW, hT, sq, rstd)
        for j0 in range(0, NCOL, 128):
            m = min(128, NCOL - j0)
            pb = C.bank()
            for c in range(8):
                P.op('pe', (lambda c, j0, m, pb, hT: lambda e: e.matmul(pb[0:m, :], lhsT=Wb[:, c, j0:j0 + m], rhs=hT[:, c, :], start=(c == 0), stop=(c == 7)))(c, j0, m, pb, hT), r=[Wb, hT], w=[pb])
            ob = obs[oi % 4]
            eng = 'act' if oi % 2 == 0 else 'dve'
            if eng == 'act':
                P.op('act', (lambda ob, pb, m: lambda e: e.activation(out=ob[0:m, :], in_=pb[0:m, :], func=AF.Copy))(ob, pb, m), r=[pb], w=[ob])
            else:
                P.op('dve', (lambda ob, pb, m: lambda e: e.tensor_copy(out=ob[0:m, :], in_=pb[0:m, :]))(ob, pb, m), r=[pb], w=[ob])
            P.dma('pool', (lambda ob, j0, m, t0: lambda e: e.dma_start(out=uT[j0:j0 + m, t0:t0 + TW], in_=ob[0:m, :]))(ob, j0, m, t0), r=[ob], final=True)
            oi += 1
    return C.finish()


def build_merge(T):
    C = Ctx(); P = C.P
    xT = C.dram_in("xT", [D, T]); gain = C.dram_in("gain", [D]); Wg = C.dram_in("Wg", [D, 2 * D])
    oaT = C.dram_in("oaT", [512, T]); obT = C.dram_in("obT", [512, T])
    PA = C.dram_in("PA", [512, D]); PB = C.dram_in("PB", [512, D]); Wo = C.dram_in("Wo", [D, D])
    x1T = C.dram_out("x1T", [D, T])
    C.consts(); C.psbanks()
    gain_t = C.load_vec_t("gain_t", gain, D)
    Wgb = C.load_w_bf16("Wgb", Wg, D, 2 * D)
    PAb = C.load_w_bf16("PAb", PA, 512, D)
    PBb = C.load_w_bf16("PBb", PB, 512, D)
    Wob = C.load_w_bf16("Wob", Wo, D, D)
    TW = 512
    xt = C.sb("xt", [128, 8, TW]); sq = C.sb("sq", [128, 8, TW], BF16); rstd = C.sb("rstd", [128, TW])
    hT = C.sb("hT", [128, 8, TW], BF16)
    oaf = C.sb("oaf", [128, 4, TW]); obf = C.sb("obf", [128, 4, TW])
    oab = C.sb("oab", [128, 4, TW], BF16); obb = C.sb("obb", [128, 4, TW], BF16)
    yb = C.sb("yb", [128, 8, TW], BF16)
    sga = [C.sb("sga%d" % i, [128, TW]) for i in range(2)]; sgb = [C.sb("sgb%d" % i, [128, TW]) for i in range(2)]
    t1 = [C.sb("t1_%d" % i, [128, TW]) for i in range(2)]; t2 = [C.sb("t2_%d" % i, [128, TW]) for i in range(2)]
    obs = [C.sb("ob%d" % i, [128, TW]) for i in range(2)]
    xv = xT.rearrange("(c p) t -> p c t", p=128)
    oav = oaT.rearrange("(c p) t -> p c t", p=128); obv = obT.rearrange("(c p) t -> p c t", p=128)
    for ti, t0 in enumerate(range(0, T, TW)):
        P.dma('sp', (lambda t0: lambda e: e.dma_start(out=xt[:], in_=xv[:, :, t0:t0 + TW]))(t0), w=[xt])
        P.dma('sp', (lambda t0: lambda e: e.dma_start(out=oaf[:], in_=oav[:, :, t0:t0 + TW]))(t0), w=[oaf])
        P.dma('sp', (lambda t0: lambda e: e.dma_start(out=obf[:], in_=obv[:, :, t0:t0 + TW]))(t0), w=[obf])
        P.op('pool', lambda e: e.tensor_copy(out=oab[:], in_=oaf[:]), r=[oaf], w=[oab])
        P.op('pool', lambda e: e.tensor_copy(out=obb[:], in_=obf[:]), r=[obf], w=[obb])
        C.rmsnorm(xt, gain_t, TW, hT, sq, rstd)
        for d in range(8):
            ds = slice(d * 128, (d + 1) * 128)
            pya = C.bank(); pga = C.bank(); pyb = C.bank(); pgb = C.bank()
            for k in range(4):
                P.op('pe', (lambda k, ds, pb: lambda e: e.matmul(pb[:], lhsT=PAb[:, k, ds], rhs=oab[:, k, :], start=(k == 0), stop=(k == 3)))(k, ds, pya), r=[PAb, oab], w=[pya])
            for k in range(8):
                P.op('pe', (lambda k, ds, pb: lambda e: e.matmul(pb[:], lhsT=Wgb[:, k, ds], rhs=hT[:, k, :], start=(k == 0), stop=(k == 7)))(k, ds, pga), r=[Wgb, hT], w=[pga])
            for k in range(4):
                P.op('pe', (lambda k, ds, pb: lambda e: e.matmul(pb[:], lhsT=PBb[:, k, ds], rhs=obb[:, k, :], start=(k == 0), stop=(k == 3)))(k, ds, pyb), r=[PBb, obb], w=[pyb])
            ds2 = slice(D + d * 128, D + (d + 1) * 128)
            for k in range(8):
                P.op('pe', (lambda k, ds2, pb: lambda e: e.matmul(pb[:], lhsT=Wgb[:, k, ds2], rhs=hT[:, k, :], start=(k == 0), stop=(k == 7)))(k, ds2, pgb), r=[Wgb, hT], w=[pgb])
            sa = sga[d % 2]; sb_ = sgb[d % 2]; ta = t1[d % 2]; tb = t2[d % 2]
            P.op('act', (lambda sa, pga: lambda e: e.activation(out=sa[:], in_=pga[:], func=AF.Sigmoid))(sa, pga), r=[pga], w=[sa])
            P.op('act', (lambda sb_, pgb: lambda e: e.activation(out=sb_[:], in_=pgb[:], func=AF.Sigmoid))(sb_, pgb), r=[pgb], w=[sb_])
            P.op('dve', (lambda ta, sa, pya: lambda e: e.tensor_tensor(out=ta[:], in0=pya[:], in1=sa[:], op=ALU.mult))(ta, sa, pya), r=[pya, sa], w=[ta])
            P.op('dve', (lambda tb, sb_, pyb: lambda e: e.tensor_tensor(out=tb[:], in0=pyb[:], in1=sb_[:], op=ALU.mult))(tb, sb_, pyb), r=[pyb, sb_], w=[tb])
            P.op('pool', (lambda d, ta, tb: lambda e: e.tensor_tensor(out=yb[:, d, :], in0=ta[:], in1=tb[:], op=ALU.add))(d, ta, tb), r=[ta, tb], w=[yb])
        for d in range(8):
            ds = slice(d * 128, (d + 1) * 128)
            pb = C.bank()
            for k in range(8):
                P.op('pe', (lambda k, ds, pb: lambda e: e.matmul(pb[:], lhsT=Wob[:, k, ds], rhs=yb[:, k, :], start=(k == 0), stop=(k == 7)))(k, ds, pb), r=[Wob, yb], w=[pb])
            ob = obs[d % 2]
            P.op('dve', (lambda d, ob, pb: lambda e: e.tensor_tensor(out=ob[:], in0=pb[:], in1=xt[:, d, :], op=ALU.add))(d, ob, pb), r=[pb, xt], w=[ob])
            P.dma('pool', (lambda ob, d, t0: lambda e: e.dma_start(out=x1T[d * 128:(d + 1) * 128, t0:t0 + TW], in_=ob[:]))(ob, d, t0), r=[ob], final=True)
    return C.finish()


DFF = 2816
HW = 32
def build_ffn(T, HALO=True):
    C = Ctx(); P = C.P
    xT = C.dram_in("xT", [D, T + HW]); gain = C.dram_in("gain", [D]); Wup = C.dram_in("Wup", [D, 2 * DFF])
    cw = C.dram_in("cw", [3, 2 * DFF]); Wd = C.dram_in("Wd", [DFF, D])
    x2T = C.dram_out("x2T", [D, T])
    C.consts(); C.psbanks()
    gain_t = C.load_vec_t("gain_t", gain, D)
    cwt = C.sb("cwt", [128, 3, 44])
    for j in range(3):
        P.dma('sp', (lambda j: lambda e: e.dma_start(out=cwt[:, j, :], in_=cw[j, :].rearrange("(c p) -> p c", p=128), allow_slow_non_contiguous=True))(j), w=[cwt])
    Wub = C.load_w_bf16("Wub", Wup, D, 2 * DFF, SC=1408)
    Wdb = C.load_w_bf16("Wdb", Wd, DFF, D, SC=1408)
    TW = 256
    xt = C.sb("xt", [128, 8, TW]); sq = C.sb("sq", [128, 8, TW], BF16); rstd = C.sb("rstd", [128, TW])
    hT = C.sb("hT", [128, 8, TW], BF16)
    gb = C.sb("gb", [128, 22, TW], BF16)
    sta = C.sb("sta", [128, 22, 2]); stb = C.sb("stb", [128, 22, 2])
    P.op('pool', lambda e: e.memset(sta[:], 0.0), w=[sta])
    P.op('pool', lambda e: e.memset(stb[:], 0.0), w=[stb])
    ua = [C.sb("ua%d" % i, [128, TW + 2]) for i in range(2)]; ub = [C.sb("ub%d" % i, [128, TW + 2]) for i in range(2)]
    ca = [C.sb("ca%d" % i, [128, TW]) for i in range(2)]; cb = [C.sb("cb%d" % i, [128, TW]) for i in range(2)]
    sa = [C.sb("sa%d" % i, [128, TW]) for i in range(2)]
    obs = [C.sb("ob%d" % i, [128, TW]) for i in range(2)]
    xv = xT.rearrange("(c p) t -> p c t", p=128)
    tiles = ([(0, HW)] if HALO else [(0, 0)]) + [(HW + t0, TW) for t0 in range(0, T, TW)]
    def do_tile(ti, c0, w):
        P.dma('sp', (lambda c0, w: lambda e: e.dma_start(out=xt[:, :, 0:w], in_=xv[:, :, c0:c0 + w]))(c0, w), w=[xt])
        C.rmsnorm(xt, gain_t, w, hT, sq, rstd)
        for i in range(22):
            pa = C.bank(); pb = C.bank()
            for k in range(8):
                P.op('pe', (lambda k, i, pa: lambda e: e.matmul(pa[:, 0:w], lhsT=Wub[:, k, i * 128:(i + 1) * 128], rhs=hT[:, k, 0:w], start=(k == 0), stop=(k == 7)))(k, i, pa), r=[Wub, hT], w=[pa])
            for k in range(8):
                P.op('pe', (lambda k, i, pb: lambda e: e.matmul(pb[:, 0:w], lhsT=Wub[:, k, DFF + i * 128:DFF + (i + 1) * 128], rhs=hT[:, k, 0:w], start=(k == 0), stop=(k == 7)))(k, i, pb), r=[Wub, hT], w=[pb])
            for (pp, uu, stt, cc, fo, eng) in ((pa, ua[i % 2], sta, ca[i % 2], i, 'dve'), (pb, ub[i % 2], stb, cb[i % 2], 22 + i, 'dve')):
                P.op('act', (lambda pp, uu: lambda e: e.activation(out=uu[:, 2:2 + w], in_=pp[:, 0:w], func=AF.Copy))(pp, uu), r=[pp], w=[uu])
                P.op(eng, (lambda uu, stt, i: lambda e: e.tensor_copy(out=uu[:, 0:2], in_=stt[:, i, :]))(uu, stt, i), r=[stt], w=[uu])
                P.op(eng, (lambda uu, stt, i: lambda e: e.tensor_copy(out=stt[:, i, :], in_=uu[:, w:w + 2]))(uu, stt, i), r=[uu], w=[stt])
                P.op(eng, (lambda uu, cc, fo: lambda e: e.tensor_scalar(out=cc[:, 0:w], in0=uu[:, 0:w], scalar1=cwt[:, 0, fo:fo + 1], scalar2=None, op0=ALU.mult))(uu, cc, fo), r=[uu, cwt], w=[cc])
                P.op(eng, (lambda uu, cc, fo: lambda e: e.scalar_tensor_tensor(out=cc[:, 0:w], in0=uu[:, 1:1 + w], scalar=cwt[:, 1, fo:fo + 1], in1=cc[:, 0:w], op0=ALU.mult, op1=ALU.add))(uu, cc, fo), r=[uu, cwt, cc], w=[cc])
                P.op(eng, (lambda uu, cc, fo: lambda e: e.scalar_tensor_tensor(out=cc[:, 0:w], in0=uu[:, 2:2 + w], scalar=cwt[:, 2, fo:fo + 1], in1=cc[:, 0:w], op0=ALU.mult, op1=ALU.add))(uu, cc, fo), r=[uu, cwt, cc], w=[cc])
            if ti == 0:
                continue
            s_ = sa[i % 2]
            P.op('act', (lambda s_, cc: lambda e: e.activation(out=s_[:, 0:w], in_=cc[:, 0:w], func=AF.Silu))(s_, ca[i % 2]), r=[ca[i % 2]], w=[s_])
            P.op('dve', (lambda s_, cc, i: lambda e: e.tensor_tensor(out=gb[:, i, 0:w], in0=s_[:, 0:w], in1=cc[:, 0:w], op=ALU.mult))(s_, cb[i % 2], i), r=[s_, cb[i % 2]], w=[gb])
        if ti == 0:
            return
        for d in range(8):
            pb = C.bank()
            for i in range(22):
                P.op('pe', (lambda i, d, pb: lambda e: e.matmul(pb[:, 0:w], lhsT=Wdb[:, i, d * 128:(d + 1) * 128], rhs=gb[:, i, 0:w], start=(i == 0), stop=(i == 21)))(i, d, pb), r=[Wdb, gb], w=[pb])
            ob = obs[d % 2]
            P.op('dve', (lambda d, ob, pb: lambda e: e.tensor_tensor(out=ob[:, 0:w], in0=pb[:, 0:w], in1=xt[:, d, 0:w], op=ALU.add))(d, ob, pb), r=[pb, xt], w=[ob])
            P.dma('pool', (lambda ob, d, c0: lambda e: e.dma_start(out=x2T[d * 128:(d + 1) * 128, c0 - HW:c0 - HW + w], in_=ob[:, 0:w]))(ob, d, c0), r=[ob], final=True)
    for ti, (c0, w) in enumerate(tiles):
        if w > 0:
            do_tile(ti, c0, w)
    return C.finish()


LNX_EPS = 1e-5 * 64
PV = dict(mu_r=0, mu_k=1, mu_v=2, w0=3, a0=4, k_k=5, k_a=6, r_k=7, ln_w=8, ln_b=9, v0=10, mu_g=11, mu_w=12, mu_a=13, mu_va=14)
NPV = 18


def build_rwkv(T, layer0, same_sync=False, G=8):
    C = Ctx(); P = C.P
    P.same_engine_sync = same_sync
    zr = C.dram_in("zr", [128, T]); zk = C.dram_in("zk", [128, T]); zvo = C.dram_in("zvo", [128, T])
    zw = C.dram_in("zw", [64, T]); za = C.dram_in("za", [64, T]); zg = C.dram_in("zg", [128, T])
    pvec = C.dram_in("pvec", [128, NPV])
    w_up = C.dram_in("w_up", [64, 128]); a_up = C.dram_in("a_up", [64, 128]); g_up = C.dram_in("g_up", [128, 128])
    if not layer0:
        zva = C.dram_in("zva", [512, T]); v1 = C.dram_in("v1", [512, 32]); v2 = C.dram_in("v2", [32, 128]); vfirst = C.dram_in("vfirst", [128, T])
    oT = C.dram_out("oT", [128, T])
    if layer0:
        vfo = C.dram_out("vfo", [128, T])
    SCR = {n: C.dram_tmp("scr_" + n, [128, T]) for n in ("W", "A", "B", "K", "R", "V", "G", "BV")}
    C.psbanks()
    pv = C.sb("pv", [128, NPV]); P.d('sp', w=[pv], out=pv[:], in_=pvec[:, :])
    wup_t = C.sb("wup_t", [64, 128]); P.d('sp', w=[wup_t], out=wup_t[:], in_=w_up[:, :])
    aup_t = C.sb("aup_t", [64, 128]); P.d('sp', w=[aup_t], out=aup_t[:], in_=a_up[:, :])
    gup_t = C.sb("gup_t", [128, 128]); P.d('sp', w=[gup_t], out=gup_t[:], in_=g_up[:, :])
    if not layer0:
        v1_t = C.sb("v1_t", [128, 4, 32]); P.d('sp', w=[v1_t], out=v1_t[:], in_=v1.rearrange("(c p) r -> p c r", p=128))
        v2_t = C.sb("v2_t", [32, 128]); P.d('sp', w=[v2_t], out=v2_t[:], in_=v2[:, :])
    I2 = C.sb("I2", [128, 64]); bo = C.sb("bo", [128, 128])
    P.i('pool', 'memset', w=[I2], ap=I2[:], constant=1.0)
    for h in range(2):
        P.i('pool', 'affine_select', r=[I2], w=[I2], out=I2[h * 64:(h + 1) * 64, :], in_=I2[h * 64:(h + 1) * 64, :], pattern=[[-1, 64]], compare_op=ALU.is_equal, fill=0.0, base=0, channel_multiplier=1)
    P.i('pool', 'memset', w=[bo], ap=bo[:], constant=0.0)
    P.i('pool', 'memset', w=[bo], ap=bo[0:64, 0:64], constant=1.0)
    P.i('pool', 'memset', w=[bo], ap=bo[64:128, 64:128], constant=1.0)

    def col(n, rows=128, off=0):
        return pv[0:rows, PV[n] + off:PV[n] + off + 1]

    TP = 512
    NB = 2
    zt = {n: [C.sb("z_%s%d" % (n, i), [rows, TP + 1]) for i in range(NB)] for n, rows in (("r", 128), ("k", 128), ("vo", 128), ("w", 64), ("a", 64), ("g", 128))}
    zs = {n: C.sb("zs_" + n, [rows, TP]) for n, rows in (("r", 128), ("k", 128), ("vo", 128), ("w", 64), ("a", 64), ("g", 128))}
    if not layer0:
        zva_t = [C.sb("zva_t%d" % i, [128, 4, TP + 1]) for i in range(NB)]
        zva_s = C.sb("zva_s", [128, 4, TP])
        vf_t = [C.sb("vf_t%d" % i, [128, TP]) for i in range(NB)]
        lr = C.sb("lr", [32, TP]); vm = C.sb("vm", [128, TP])
    tmp = C.sb("tmp", [128, TP]); tmp2 = C.sb("tmp2", [128, TP])
    wdec = C.sb("wdec", [128, TP]); aa = C.sb("aa", [128, TP]); gg = C.sb("gg", [128, TP]); vv = C.sb("vv", [128, TP])
    kk = C.sb("kk", [128, TP]); kn = C.sb("kn", [128, TP]); at_ = C.sb("at_", [128, TP]); bt_ = C.sb("bt_", [128, TP]); kp = C.sb("kp", [128, TP]); bv = C.sb("bv", [128, TP])
    srcs = dict(r=zr, k=zk, vo=zvo, w=zw, a=za, g=zg)
    mus = dict(r=("mu_r", 128), k=("mu_k", 128), vo=("mu_v", 128), w=("mu_w", 64), a=("mu_a", 64), g=("mu_g", 128))

    def prep_tile(ti, t0):
        par = ti % NB
        for n, src in srcs.items():
            z = zt[n][par]; rows = mus[n][1]
            if t0 == 0:
                P.i('pool', 'memset', w=[z], ap=z[:, 0:1], constant=0.0)
                P.d('sp', w=[z], out=z[:, 1:TP + 1], in_=src[:, 0:TP])
            else:
                P.d('sp', w=[z], out=z[:], in_=src[:, t0 - 1:t0 + TP])
            P.i('dve', 'tensor_tensor', r=[z], w=[tmp], out=tmp[0:rows, :], in0=z[:, 0:TP], in1=z[:, 1:TP + 1], op=ALU.subtract)
            P.i('dve', 'scalar_tensor_tensor', r=[tmp, z, pv], w=[zs[n]], out=zs[n][:], in0=tmp[0:rows, :], scalar=col(mus[n][0], rows), in1=z[:, 1:TP + 1], op0=ALU.mult, op1=ALU.add)
        if not layer0:
            z = zva_t[par]
            if t0 == 0:
                P.i('pool', 'memset', w=[z], ap=z[:, :, 0:1], constant=0.0)
                P.d('sp', w=[z], out=z[:, :, 1:TP + 1], in_=zva.rearrange("(c p) t -> p c t", p=128)[:, :, 0:TP])
            else:
                P.d('sp', w=[z], out=z[:], in_=zva.rearrange("(c p) t -> p c t", p=128)[:, :, t0 - 1:t0 + TP])
            P.d('sp', w=[vf_t[par]], out=vf_t[par][:], in_=vfirst[:, t0:t0 + TP])
            for c in range(4):
                P.i('dve', 'tensor_tensor', r=[z], w=[tmp], out=tmp[:], in0=z[:, c, 0:TP], in1=z[:, c, 1:TP + 1], op=ALU.subtract)
                P.i('dve', 'scalar_tensor_tensor', r=[tmp, z, pv], w=[zva_s], out=zva_s[:, c, :], in0=tmp[:], scalar=col("mu_va", 128, c), in1=z[:, c, 1:TP + 1], op0=ALU.mult, op1=ALU.add)
        sl = slice(t0, t0 + TP)
        P.i('act', 'activation', r=[zs["w"]], w=[tmp2], out=tmp2[0:64, :], in_=zs["w"][:], func=AF.Tanh)
        pb = C.bank()
        P.i('pe', 'matmul', r=[wup_t, tmp2], w=[pb], out=pb[:], lhsT=wup_t[:], rhs=tmp2[0:64, :], start=True, stop=True)
        P.i('act', 'activation', r=[pb, pv], w=[wdec], out=wdec[:], in_=pb[:], func=AF.Sigmoid, bias=col("w0"))
        P.i('act', 'activation', r=[wdec], w=[wdec], out=wdec[:], in_=wdec[:], func=AF.Exp, scale=-math.exp(-0.5))
        P.d('pool', r=[wdec], w=[SCR["W"]], out=SCR["W"][:, sl], in_=wdec[:])
        pb = C.bank()
        P.i('pe', 'matmul', r=[aup_t, zs["a"]], w=[pb], out=pb[:], lhsT=aup_t[:], rhs=zs["a"][:], start=True, stop=True)
        P.i('act', 'activation', r=[pb, pv], w=[aa], out=aa[:], in_=pb[:], func=AF.Sigmoid, bias=col("a0"))
        P.i('act', 'activation', r=[zs["g"]], w=[tmp2], out=tmp2[:], in_=zs["g"][:], func=AF.Sigmoid)
        pb = C.bank()
        P.i('pe', 'matmul', r=[gup_t, tmp2], w=[pb], out=pb[:], lhsT=gup_t[:], rhs=tmp2[:], start=True, stop=True)
        P.i('act', 'activation', r=[pb], w=[gg], out=gg[:], in_=pb[:], func=AF.Copy)
        P.d('pool', r=[gg], w=[SCR["G"]], out=SCR["G"][:, sl], in_=gg[:])
        if layer0:
            P.i('pool', 'tensor_copy', r=[zs["vo"]], w=[vv], out=vv[:], in_=zs["vo"][:])
            P.d('pool', r=[vv], out=vfo[:, sl], in_=vv[:], final=True)
        else:
            pb = C.bank()
            for c in range(4):
                P.i('pe', 'matmul', r=[v1_t, zva_s], w=[pb], out=pb[0:32, :], lhsT=v1_t[:, c, :], rhs=zva_s[:, c, :], start=(c == 0), stop=(c == 3))
            P.i('act', 'activation', r=[pb], w=[lr], out=lr[:], in_=pb[0:32, :], func=AF.Copy)
            pb2 = C.bank()
            P.i('pe', 'matmul', r=[v2_t, lr], w=[pb2], out=pb2[:], lhsT=v2_t[:], rhs=lr[:], start=True, stop=True)
            P.i('act', 'activation', r=[pb2, pv], w=[vm], out=vm[:], in_=pb2[:], func=AF.Sigmoid, bias=col("v0"))
            P.i('dve', 'tensor_tensor', r=[vf_t[par], zs["vo"]], w=[tmp], out=tmp[:], in0=vf_t[par][:], in1=zs["vo"][:], op=ALU.subtract)
            P.i('dve', 'tensor_tensor', r=[tmp, vm], w=[tmp], out=tmp[:], in0=tmp[:], in1=vm[:], op=ALU.mult)
            P.i('dve', 'tensor_tensor', r=[tmp, zs["vo"]], w=[vv], out=vv[:], in0=tmp[:], in1=zs["vo"][:], op=ALU.add)
        P.d('pool', r=[vv], w=[SCR["V"]], out=SCR["V"][:, sl], in_=vv[:])
        P.i('dve', 'tensor_scalar', r=[zs["k"], pv], w=[kk], out=kk[:], in0=zs["k"][:], scalar1=col("k_k"), scalar2=None, op0=ALU.mult)
        P.i('act', 'activation', r=[kk], w=[tmp2], out=tmp2[:], in_=kk[:], func=AF.Square)
        pb = C.bank()
        P.i('pe', 'matmul', r=[bo, tmp2], w=[pb], out=pb[:], lhsT=bo[:], rhs=tmp2[:], start=True, stop=True)
        P.i('act', 'activation', r=[pb], w=[kn], out=kn[:], in_=pb[:], func=AF.Sqrt)
        P.i('dve', 'tensor_scalar', r=[kn], w=[kn], out=kn[:], in0=kn[:], scalar1=1e-12, scalar2=None, op0=ALU.max)
        P.i('dve', 'reciprocal', r=[kn], w=[kn], out=kn[:], in_=kn[:])
        P.i('dve', 'tensor_tensor', r=[kk, kn], w=[kn], out=kn[:], in0=kk[:], in1=kn[:], op=ALU.mult)
        P.i('dve', 'tensor_scalar', r=[kn], w=[at_], out=at_[:], in0=kn[:], scalar1=-1.0, scalar2=None, op0=ALU.mult)
        P.d('pool', r=[at_], w=[SCR["A"]], out=SCR["A"][:, sl], in_=at_[:])
        P.i('dve', 'tensor_tensor', r=[kn, aa], w=[bt_], out=bt_[:], in0=kn[:], in1=aa[:], op=ALU.mult)
        P.d('pool', r=[bt_], w=[SCR["B"]], out=SCR["B"][:, sl], in_=bt_[:])
        P.i('dve', 'tensor_scalar', r=[aa, pv], w=[tmp], out=tmp[:], in0=aa[:], scalar1=-1.0, scalar2=col("k_a"), op0=ALU.add, op1=ALU.mult)
        P.i('dve', 'scalar_tensor_tensor', r=[tmp, zs["k"]], w=[kp], out=kp[:], in0=tmp[:], scalar=1.0, in1=zs["k"][:], op0=ALU.add, op1=ALU.mult)
        P.d('pool', r=[kp], w=[SCR["K"]], out=SCR["K"][:, sl], in_=kp[:])
        P.d('pool', r=[zs["r"]], w=[SCR["R"]], out=SCR["R"][:, sl], in_=zs["r"][:])
        P.i('dve', 'scalar_tensor_tensor', r=[zs["r"], kp, pv], w=[tmp2], out=tmp2[:], in0=zs["r"][:], scalar=col("r_k"), in1=kp[:], op0=ALU.mult, op1=ALU.mult)
        pb = C.bank()
        P.i('pe', 'matmul', r=[bo, tmp2], w=[pb], out=pb[:], lhsT=bo[:], rhs=tmp2[:], start=True, stop=True)
        P.i('dve', 'tensor_tensor', r=[pb, vv], w=[bv], out=bv[:], in0=pb[:], in1=vv[:], op=ALU.mult)
        P.d('pool', r=[bv], w=[SCR["BV"]], out=SCR["BV"][:, sl], in_=bv[:])

    for ti, t0 in enumerate(range(0, T, TP)):
        prep_tile(ti, t0)

    TC = 512
    S = C.sb("S", [128, 64]); Y = C.sb("Y", [128, T]); junk = C.sb("junk", [128, 64]); sa = C.sb("sa", [128, 1])
    P.i('dve', 'memset', w=[S], ap=S[:], constant=0.0)
    types = ("W", "A", "B", "K", "R")
    ch = {n: [C.sb("ch_%s%d" % (n, i), [128, TC]) for i in range(2)] for n in types + ("V",)}
    E = {n: [C.sb("E_%s%d" % (n, i), [128, G, 64]) for i in range(2)] for n in types}
    BC = {n: [C.sb("BC_%s%d" % (n, i), [128, G * 64]) for i in range(2)] for n in types}
    gi = 0
    for ci, c0 in enumerate(range(0, T, TC)):
        cp = ci % 2
        for n in types + ("V",):
            P.d('sp', r=[SCR[n]], w=[ch[n][cp]], out=ch[n][cp][:], in_=SCR[n][:, c0:c0 + TC])
        for g0 in range(0, TC, G):
            gp = gi % 2; gi += 1
            for n in types:
                src = ch[n][cp]
                P.i('pool', 'tensor_tensor', r=[I2, src], w=[E[n][gp]], out=E[n][gp][:], in0=I2[:].unsqueeze(1).broadcast_to([128, G, 64]), in1=src[:, g0:g0 + G].unsqueeze(2).broadcast_to([128, G, 64]), op=ALU.mult)
                pb = C.bank()
                P.i('pe', 'matmul', r=[bo, E[n][gp]], w=[pb], out=pb[:, 0:G * 64], lhsT=bo[:], rhs=E[n][gp][:].rearrange("p a b -> p (a b)"), start=True, stop=True)
                P.i('act', 'activation', r=[pb], w=[BC[n][gp]], out=BC[n][gp][:], in_=pb[:, 0:G * 64], func=AF.Copy)
            vch = ch["V"][cp]
            for j in range(G):
                t = c0 + g0 + j
                js = slice(j * 64, (j + 1) * 64)
                P.i('dve', 'scalar_tensor_tensor', r=[S, BC["A"][gp]], w=[junk, sa], out=junk[:], in0=S[:], scalar=1.0, in1=BC["A"][gp][:, js], op0=ALU.mult, op1=ALU.mult, accum_out=sa[:, 0:1])
                P.i('dve', 'tensor_tensor', r=[S, BC["W"][gp]], w=[S], out=S[:], in0=S[:], in1=BC["W"][gp][:, js], op=ALU.mult)
                P.i('dve', 'scalar_tensor_tensor', r=[S, BC["B"][gp], sa], w=[S], out=S[:], in0=BC["B"][gp][:, js], scalar=sa[:, 0:1], in1=S[:], op0=ALU.mult, op1=ALU.add)
                P.i('dve', 'scalar_tensor_tensor', r=[S, BC["K"][gp], vch], w=[S], out=S[:], in0=BC["K"][gp][:, js], scalar=vch[:, g0 + j:g0 + j + 1], in1=S[:], op0=ALU.mult, op1=ALU.add)
                P.i('dve', 'scalar_tensor_tensor', r=[S, BC["R"][gp]], w=[junk, Y], out=junk[:], in0=S[:], scalar=1.0, in1=BC["R"][gp][:, js], op0=ALU.mult, op1=ALU.mult, accum_out=Y[:, t:t + 1])

    gt = [wdec, aa]; bvt = [gg, vv]
    yc = kk; ysq = kn; rs = tmp; oo = [at_, bt_]
    for ti, t0 in enumerate(range(0, T, TP)):
        par = ti % 2
        sl = slice(t0, t0 + TP)
        P.d('sp', r=[SCR["G"]], w=[gt[par]], out=gt[par][:], in_=SCR["G"][:, sl])
        P.d('sp', r=[SCR["BV"]], w=[bvt[par]], out=bvt[par][:], in_=SCR["BV"][:, sl])
        pb = C.bank()
        P.i('pe', 'matmul', r=[bo, Y], w=[pb], out=pb[:], lhsT=bo[:], rhs=Y[:, sl], start=True, stop=True)
        P.i('dve', 'scalar_tensor_tensor', r=[pb, Y], w=[yc], out=yc[:], in0=pb[:], scalar=-1.0 / 64, in1=Y[:, sl], op0=ALU.mult, op1=ALU.add)
        P.i('act', 'activation', r=[yc], w=[ysq], out=ysq[:], in_=yc[:], func=AF.Square)
        pb2 = C.bank()
        P.i('pe', 'matmul', r=[bo, ysq], w=[pb2], out=pb2[:], lhsT=bo[:], rhs=ysq[:], start=True, stop=True)
        P.i('dve', 'tensor_scalar', r=[pb2], w=[rs], out=rs[:], in0=pb2[:], scalar1=1.0 / 64, scalar2=LNX_EPS, op0=ALU.mult, op1=ALU.add)
        P.i('act', 'activation', r=[rs], w=[rs], out=rs[:], in_=rs[:], func=AF.Sqrt)
        P.i('dve', 'reciprocal', r=[rs], w=[rs], out=rs[:], in_=rs[:])
        P.i('dve', 'tensor_tensor', r=[yc, rs], w=[yc], out=yc[:], in0=yc[:], in1=rs[:], op=ALU.mult)
        P.i('dve', 'tensor_scalar', r=[yc, pv], w=[yc], out=yc[:], in0=yc[:], scalar1=col("ln_w"), scalar2=col("ln_b"), op0=ALU.mult, op1=ALU.add)
        P.i('dve', 'tensor_tensor', r=[yc, bvt[par]], w=[yc], out=yc[:], in0=yc[:], in1=bvt[par][:], op=ALU.add)
        o = oo[par]
        P.i('dve', 'tensor_tensor', r=[yc, gt[par]], w=[o], out=o[:], in0=yc[:], in1=gt[par][:], op=ALU.mult)
        P.d('pool', r=[o], out=oT[:, sl], in_=o[:], final=True)
    return C.finish()


NEGB = -30000.0
SCALE = 0.125
EPSN = 1e-6


def build_nsa(T, NBLK=None, DBG=False):
    C = Ctx(); P = C.P
    P.same_engine_sync = True
    nblk = T // 128
    NBLK = nblk if NBLK is None else NBLK
    NCMP = T // 16
    NCT = NCMP // 128
    NS = T // 64
    assert NS <= 128
    qT = C.dram_in("qT", [256, T]); kcT = C.dram_in("kcT", [64, T]); vcT = C.dram_in("vcT", [64, T])
    ksT = C.dram_in("ksT", [64, T]); kwT = C.dram_in("kwT", [64, T])
    vs = C.dram_in("vs", [T, 64]); vw = C.dram_in("vw", [T, 64]); gl = C.dram_in("gl", [T, 6])
    gain = C.dram_in("gain", [4, 64]); pe = C.dram_in("pe", [2, 32, 64]); w1 = C.dram_in("w1", [2, 32, 64, 64]); w2 = C.dram_in("w2", [2, 64, 64])
    out = C.dram_out("o", [T, 128])

    psS = [C.ps("psS%d" % i, [128, 512]) for i in range(3)]
    psSlc = C.ps("psSlc", [128, 512]); psWin = C.ps("psWin", [128, 512])
    psC = [C.ps("psC%d" % i, [128, 512]) for i in range(2)]
    psT = C.ps("psT", [128, 512])
    sidx = [0]
    def sbank():
        b = psS[sidx[0] % 3]; sidx[0] += 1
        return b

    ident = C.sb("ident", [128, 128])
    P.i('pool', 'memset', w=[ident], ap=ident[:], constant=1.0)
    P.i('pool', 'affine_select', r=[ident], w=[ident], out=ident[:], in_=ident[:], pattern=[[-1, 128]], compare_op=ALU.is_equal, fill=0.0, base=0, channel_multiplier=1)
    ones64 = C.sb("ones64", [64, 64])
    P.i('pool', 'memset', w=[ones64], ap=ones64[:], constant=1.0)
    Ef = C.sb("Ef", [128, 2048]); Ebig = C.sb("Ebig", [128, T], BF16)
    for k0 in range(0, T, 2048):
        P.i('pool', 'memset', w=[Ef], ap=Ef[:], constant=1.0)
        P.i('pool', 'affine_select', r=[Ef], w=[Ef], out=Ef[:], in_=Ef[:], pattern=[[1, 2048]], compare_op=ALU.is_ge, fill=0.0, base=k0, channel_multiplier=-64)
        P.i('pool', 'affine_select', r=[Ef], w=[Ef], out=Ef[:], in_=Ef[:], pattern=[[-1, 2048]], compare_op=ALU.is_ge, fill=0.0, base=63 - k0, channel_multiplier=64)
        P.i('pool', 'tensor_copy', r=[Ef], w=[Ebig], out=Ebig[:, k0:k0 + 2048], in_=Ef[:])
    Fbase = C.sb("Fbase", [128, 256])
    P.i('pool', 'memset', w=[Fbase], ap=Fbase[:], constant=0.0)
    P.i('pool', 'memset', w=[Fbase], ap=Fbase[0:64, 128:129], constant=2e4)
    P.i('pool', 'memset', w=[Fbase], ap=Fbase[0:64, 127:128], constant=4e4)
    P.i('pool', 'memset', w=[Fbase], ap=Fbase[64:128, 129:130], constant=2e4)
    P.i('pool', 'memset', w=[Fbase], ap=Fbase[64:128, 128:129], constant=4e4)
    gT = C.sb("gT", [64, 4]); P.d('sp', w=[gT], out=gT[:], in_=gain.rearrange("j d -> d j"), allow_slow_non_contiguous=True)
    peT = C.sb("peT", [64, 2, 32]); P.d('sp', w=[peT], out=peT[:], in_=pe.rearrange("a l d -> d a l"), allow_slow_non_contiguous=True)
    w1b = C.sb("w1b", [64, 2, 32, 64], BF16); w2b = C.sb("w2b", [64, 2, 64], BF16)
    big = C.sb("big", [128, T])
    for a in range(2):
        P.d('sp', w=[big], out=big[0:64, 0:2048].rearrange("d (l e) -> d l e", l=32), in_=w1[a].rearrange("l d e -> d l e"))
        P.i('dve', 'tensor_copy', r=[big], w=[w1b], out=w1b[:, a, :, :], in_=big[0:64, 0:2048].rearrange("d (l e) -> d l e", l=32))
    P.d('sp', w=[big], out=big[0:64, 0:128].rearrange("d (a f) -> d a f", a=2), in_=w2.rearrange("a e f -> e a f"))
    P.i('dve', 'tensor_copy', r=[big], w=[w2b], out=w2b[:], in_=big[0:64, 0:128].rearrange("d (a f) -> d a f", a=2))

    sqt = C.sb("sqt", [64, 512]); rst = C.sb("rst", [64, 512])
    def rms_fm(x_ap, W, gcol, out_ap, rkeys, wkeys):
        P.i('act', 'activation', r=rkeys, w=[sqt], out=sqt[:, 0:W], in_=x_ap, func=AF.Square)
        P.i('pe', 'matmul', r=[ones64, sqt], w=[psT], out=psT[0:64, 0:W], lhsT=ones64[:], rhs=sqt[:, 0:W], start=True, stop=True)
        P.i('dve', 'tensor_scalar', r=[psT], w=[rst], out=rst[:, 0:W], in0=psT[0:64, 0:W], scalar1=1.0 / 64, scalar2=EPSN, op0=ALU.mult, op1=ALU.add)
        P.i('act', 'activation', r=[rst], w=[rst], out=rst[:, 0:W], in_=rst[:, 0:W], func=AF.Sqrt)
        P.i('dve', 'reciprocal', r=[rst], w=[rst], out=rst[:, 0:W], in_=rst[:, 0:W])
        P.i('dve', 'scalar_tensor_tensor', r=rkeys + [rst, gT], w=wkeys, out=out_ap, in0=x_ap, scalar=gT[:, gcol:gcol + 1], in1=rst[:, 0:W], op0=ALU.mult, op1=ALU.mult)

    ksn = C.sb("ksn", [64, T], BF16); kwn = C.sb("kwn", [64, T], BF16)
    for (src, dst, gc) in ((ksT, ksn, 2), (kwT, kwn, 3)):
        P.d('sp', w=[big], out=big[0:64, :], in_=src[:, :])
        for t0 in range(0, T, 512):
            rms_fm(big[0:64, t0:t0 + 512], 512, gc, dst[:, t0:t0 + 512], [big], [dst])
    NT = T // 128
    vsa = C.sb("vsa", [128, NT, 65], BF16); vwa = C.sb("vwa", [128, NT, 65], BF16)
    for (src, dst) in ((vs, vsa), (vw, vwa)):
        P.d('sp', w=[big], out=big[:, 0:NT * 64].rearrange("p (n d) -> p n d", d=64), in_=src.rearrange("(n p) d -> p n d", p=128))
        P.i('dve', 'tensor_copy', r=[big], w=[dst], out=dst[:, :, 0:64], in_=big[:, 0:NT * 64].rearrange("p (n d) -> p n d", d=64))
        P.i('pool', 'memset', w=[dst], ap=dst[:, :, 64:65], constant=1.0)
    hidb = C.sb("hidb", [64, NCMP], BF16); kcmpT = C.sb("kcmpT", [64, NCMP], BF16)
    Vca = C.sb("Vca", [128, NCT, 193], BF16)
    tmpb = [C.sb("tmpb%d" % i, [64, 512], BF16) for i in range(2)]
    kcf = C.sb("kcf", [64, 512])
    ovf = C.sb("ovf", [128, 128])
    nvalid = NCMP - 1
    for a, src in ((0, kcT), (1, vcT)):
        P.d('sp', w=[big], out=big[0:64, :], in_=src[:, :])
        P.i('pool', 'memset', w=[hidb], ap=hidb[:], constant=0.0)
        for n0 in range(0, nvalid, 512):
            nw = min(512, nvalid - n0)
            for l in range(32):
                tb = tmpb[l % 2]
                st_ = 16 * n0 + l
                P.i('dve', 'tensor_scalar', r=[big, peT], w=[tb], out=tb[:, 0:nw], in0=big[0:64, st_:st_ + 16 * (nw - 1) + 1:16], scalar1=peT[:, a, l:l + 1], scalar2=None, op0=ALU.add)
                P.i('pe', 'matmul', r=[w1b, tb], w=[psT], out=psT[0:64, 0:nw], lhsT=w1b[:, a, l, :], rhs=tb[:, 0:nw], start=(l == 0), stop=(l == 31))
            P.i('act', 'activation', r=[psT], w=[hidb], out=hidb[:, n0:n0 + nw], in_=psT[0:64, 0:nw], func=AF.Silu)
        if a == 0:
            for n0 in range(0, NCMP, 512):
                nw = min(512, NCMP - n0)
                P.i('pe', 'matmul', r=[w2b, hidb], w=[psT], out=psT[0:64, 0:nw], lhsT=w2b[:, 0, :], rhs=hidb[:, n0:n0 + nw], start=True, stop=True)
                P.i('act', 'activation', r=[psT], w=[kcf], out=kcf[:, 0:nw], in_=psT[0:64, 0:nw], func=AF.Copy)
                rms_fm(kcf[:, 0:nw], nw, 1, kcmpT[:, n0:n0 + nw], [kcf], [kcmpT])
        else:
            for i in range(NCT):
                P.i('pe', 'matmul', r=[w2b, hidb], w=[psT], out=psT[:, 0:64], lhsT=hidb[:, i * 128:(i + 1) * 128], rhs=w2b[:, 1, :], start=True, stop=True)
                P.i('act', 'activation', r=[psT], w=[Vca], out=Vca[:, i, 0:64], in_=psT[:, 0:64], func=AF.Copy)
                P.i('pool', 'memset', w=[ovf], ap=ovf[:], constant=1.0)
                P.i('pool', 'affine_select', r=[ovf], w=[ovf], out=ovf[:], in_=ovf[:], pattern=[[-4, 128]], compare_op=ALU.is_ge, fill=0.0, base=128 * i + 1, channel_multiplier=1)
                P.i('pool', 'affine_select', r=[ovf], w=[ovf], out=ovf[:], in_=ovf[:], pattern=[[4, 128]], compare_op=ALU.is_ge, fill=0.0, base=3 - 128 * i, channel_multiplier=-1)
                P.i('pool', 'tensor_copy', r=[ovf], w=[Vca], out=Vca[:, i, 65:193], in_=ovf[:])
            P.i('pool', 'memset', w=[Vca], ap=Vca[:, :, 64:65], constant=1.0)
    gs = C.sb("gs", [128, nblk, 6])
    P.d('sp', w=[gs], out=gs[:], in_=gl.rearrange("(n p) j -> p n j", p=128))
    P.i('act', 'activation', r=[gs], w=[gs], out=gs[:], in_=gs[:], func=AF.Sigmoid)

    qf = [C.sb("qf%d" % i, [64, 4, 128]) for i in range(2)]
    qn = [C.sb("qn%d" % i, [64, 512], BF16) for i in range(2)]
    pT = [C.sb("pT%d" % i, [128, 512], BF16) for i in range(3)]
    pidx = [0]
    def pbuf():
        b = pT[pidx[0] % 3]; pidx[0] += 1
        return b
    cso = C.sb("cso", [128, 4, 193])
    rden = C.sb("rden", [128, 4]); imp = C.sb("imp", [128, 128]); imp2 = C.sb("imp2", [128, 128])
    m8 = C.sb("m8", [128, 8]); m8b = C.sb("m8b", [128, 8]); selb = C.sb("selb", [128, 128])
    selT = C.sb("selT", [128, 2, 128], BF16)
    oacc = [C.sb("oacc%d" % i, [128, 128]) for i in range(2)]
    obr = C.sb("obr", [65, 256]); otr = C.sb("otr", [128, 2, 65]); coef = C.sb("coef", [128, 2]); dd = C.sb("dd", [128, 2])
    qv = qT.rearrange("(h d) t -> d h t", d=64)

    def block(c):
        t0 = c * 128
        bp = c % 2
        q_f = qf[bp]; q_n = qn[bp]; oa = oacc[bp]
        P.d('sp', w=[q_f], out=q_f[:], in_=qv[:, :, t0:t0 + 128])
        rms_fm(q_f[:].rearrange("d h t -> d (h t)"), 512, 0, q_n[:], [q_f], [q_n])
        nmax = 8 * c + 6
        tiles = [i for i in range(NCT) if 128 * i <= nmax]
        for ii, i in enumerate(tiles):
            sb_ = sbank()
            P.i('pe', 'matmul', r=[kcmpT, q_n], w=[sb_], out=sb_[:], lhsT=kcmpT[:, i * 128:(i + 1) * 128], rhs=q_n[:], start=True, stop=True)
            p_ = pbuf()
            P.i('act', 'activation', r=[sb_], w=[p_], out=p_[:], in_=sb_[:], func=AF.Exp, scale=SCALE)
            if 8 * c < 128 * i + 129:
                P.i('pool', 'affine_select', r=[p_], w=[p_], out=p_[:].rearrange("p (h t) -> p h t", h=4), in_=p_[:].rearrange("p (h t) -> p h t", h=4), pattern=[[0, 4], [1, 128]], compare_op=ALU.is_ge, fill=0.0, base=t0 - 31 - 16 * 128 * i, channel_multiplier=-16)
            for h in range(4):
                pc = psC[h // 2]
                P.i('pe', 'matmul', r=[p_, Vca], w=[pc], out=pc[:, (h % 2) * 193:(h % 2) * 193 + 193], lhsT=p_[:, h * 128:(h + 1) * 128], rhs=Vca[:, i, :], start=(ii == 0 and h % 2 == 0), stop=(ii == len(tiles) - 1), skip_group_check=True)
        for hh in range(2):
            P.i('act', 'activation', r=[psC[hh]], w=[cso], out=cso[:, 2 * hh:2 * hh + 2, :], in_=psC[hh][:, 0:386].rearrange("p (h x) -> p h x", h=2), func=AF.Copy)
        P.i('dve', 'tensor_scalar', r=[cso], w=[rden], out=rden[:], in0=cso[:, :, 64], scalar1=1e-30, scalar2=None, op0=ALU.max)
        P.i('dve', 'reciprocal', r=[rden], w=[rden], out=rden[:], in_=rden[:])
        P.i('dve', 'scalar_tensor_tensor', r=[cso, rden, Fbase], w=[imp], out=imp[:], in0=cso[:, 0, 65:193], scalar=rden[:, 0:1], in1=Fbase[:, 128 - 2 * c:256 - 2 * c], op0=ALU.mult, op1=ALU.add)
        for h in range(1, 4):
            P.i('dve', 'scalar_tensor_tensor', r=[cso, rden, imp], w=[imp], out=imp[:], in0=cso[:, h, 65:193], scalar=rden[:, h:h + 1], in1=imp[:], op0=ALU.mult, op1=ALU.add)
        P.i('dve', 'tensor_scalar', r=[imp], w=[imp], out=imp[:, 0:1], in0=imp[:, 0:1], scalar1=1e4, scalar2=None, op0=ALU.add)
        P.i('dve', 'max', r=[imp], w=[m8], out=m8[:], in_=imp[:])
        P.i('dve', 'match_replace', r=[imp, m8], w=[imp2], out=imp2[:], in_to_replace=m8[:], in_values=imp[:], imm_value=-1e9)
        P.i('dve', 'max', r=[imp2], w=[m8b], out=m8b[:], in_=imp2[:])
        P.i('dve', 'tensor_scalar', r=[imp, m8b], w=[selb], out=selb[:], in0=imp[:], scalar1=m8b[:, 7:8], scalar2=NEGB, op0=ALU.is_lt, op1=ALU.mult)
        P.i('pe', 'transpose', r=[selb, ident], w=[psT], out=psT[:, 0:128], in_=selb[:], identity=ident[:])
        P.i('dve', 'tensor_copy', r=[psT], w=[selT], out=selT[:], in_=psT[:, 0:128].unsqueeze(1).broadcast_to([128, 2, 128]))
        P.i('dve', 'tensor_tensor', r=[gs, rden], w=[coef], out=coef[:], in0=gs[:, c, 0:6:3], in1=rden[:, 0:2], op=ALU.mult)
        for h in range(2):
            P.i('dve', 'tensor_scalar', r=[cso, coef], w=[oa], out=oa[:, h * 64:(h + 1) * 64], in0=cso[:, h, 0:64], scalar1=coef[:, h:h + 1], scalar2=None, op0=ALU.mult)
        for j in range(c + 1):
            sb_ = sbank()
            P.i('pe', 'matmul', r=[ksn, q_n], w=[sb_], out=sb_[:, 0:256], lhsT=ksn[:, j * 128:(j + 1) * 128], rhs=q_n[:, 0:256], start=True, stop=False)
            P.i('pe', 'matmul', r=[Ebig, selT], w=[sb_], out=sb_[:, 0:256], lhsT=Ebig[:, j * 128:(j + 1) * 128], rhs=selT[:].rearrange("p h t -> p (h t)"), start=False, stop=True)
            p_ = pbuf()
            P.i('act', 'activation', r=[sb_], w=[p_], out=p_[:, 0:256], in_=sb_[:, 0:256], func=AF.Exp, scale=SCALE)
            if j == c:
                P.i('pool', 'affine_select', r=[p_], w=[p_], out=p_[:, 0:256].rearrange("p (h t) -> p h t", h=2), in_=p_[:, 0:256].rearrange("p (h t) -> p h t", h=2), pattern=[[0, 2], [1, 128]], compare_op=ALU.is_ge, fill=0.0, base=0, channel_multiplier=-1)
            P.i('pe', 'matmul', r=[vsa, p_], w=[psSlc], out=psSlc[0:65, 0:256], lhsT=vsa[:, j, :], rhs=p_[:, 0:256], start=(j == 0), stop=(j == c))
        j0 = max(0, c - 4)
        for j in range(j0, c + 1):
            sb_ = sbank()
            P.i('pe', 'matmul', r=[kwn, q_n], w=[sb_], out=sb_[:, 0:256], lhsT=kwn[:, j * 128:(j + 1) * 128], rhs=q_n[:, 0:256], start=True, stop=True)
            p_ = pbuf()
            P.i('act', 'activation', r=[sb_], w=[p_], out=p_[:, 0:256], in_=sb_[:, 0:256], func=AF.Exp, scale=SCALE)
            if j == c:
                P.i('pool', 'affine_select', r=[p_], w=[p_], out=p_[:, 0:256].rearrange("p (h t) -> p h t", h=2), in_=p_[:, 0:256].rearrange("p (h t) -> p h t", h=2), pattern=[[0, 2], [1, 128]], compare_op=ALU.is_ge, fill=0.0, base=0, channel_multiplier=-1)
            if j == c - 4:
                P.i('pool', 'affine_select', r=[p_], w=[p_], out=p_[:, 0:256].rearrange("p (h t) -> p h t", h=2), in_=p_[:, 0:256].rearrange("p (h t) -> p h t", h=2), pattern=[[0, 2], [-1, 128]], compare_op=ALU.is_ge, fill=0.0, base=-1, channel_multiplier=1)
            P.i('pe', 'matmul', r=[vwa, p_], w=[psWin], out=psWin[0:65, 0:256], lhsT=vwa[:, j, :], rhs=p_[:, 0:256], start=(j == j0), stop=(j == c))
        for br, pacc in ((1, psSlc), (2, psWin)):
            P.i('act', 'activation', r=[pacc], w=[obr], out=obr[:], in_=pacc[0:65, 0:256], func=AF.Copy)
            for h in range(2):
                P.i('pe', 'transpose', r=[obr, ident], w=[psT], out=psT[:, h * 65:h * 65 + 65], in_=obr[:, h * 128:(h + 1) * 128], identity=ident[0:65, 0:65])
            P.i('act', 'activation', r=[psT], w=[otr], out=otr[:], in_=psT[:, 0:130].rearrange("p (h x) -> p h x", h=2), func=AF.Copy)
            P.i('dve', 'tensor_scalar', r=[otr], w=[dd], out=dd[:], in0=otr[:, :, 64], scalar1=1e-30, scalar2=None, op0=ALU.max)
            P.i('dve', 'reciprocal', r=[dd], w=[dd], out=dd[:], in_=dd[:])
            P.i('dve', 'tensor_tensor', r=[gs, dd], w=[coef], out=coef[:], in0=gs[:, c, br:6:3], in1=dd[:], op=ALU.mult)
            for h in range(2):
                P.i('dve', 'scalar_tensor_tensor', r=[otr, coef, oa], w=[oa], out=oa[:, h * 64:(h + 1) * 64], in0=otr[:, h, 0:64], scalar=coef[:, h:h + 1], in1=oa[:, h * 64:(h + 1) * 64], op0=ALU.mult, op1=ALU.add)
        P.d('pool', r=[oa], out=out[t0:t0 + 128, :], in_=oa[:], final=True)
        if DBG and c == 1:
            d1 = C.dram_out("d_obr", [65, 256]); P.d('sp', r=[obr], out=d1[:, :], in_=obr[:], final=True)
            d2 = C.dram_out("d_otr", [128, 130]); P.d('sp', r=[otr], out=d2[:, :], in_=otr[:].rearrange("p h x -> p (h x)"), final=True)
            d3 = C.dram_out("d_dd", [128, 2]); P.d('sp', r=[dd], out=d3[:, :], in_=dd[:], final=True)
            d4 = C.dram_out("d_coef", [128, 2]); P.d('sp', r=[coef], out=d4[:, :], in_=coef[:], final=True)
            d5 = C.dram_out("d_vwa", [128, 65], BF16); P.d('sp', r=[vwa], out=d5[:, :], in_=vwa[:, 1, :], final=True)
            d6 = C.dram_out("d_kwn", [64, 256], BF16); P.d('sp', r=[kwn], out=d6[:, :], in_=kwn[:, 0:256], final=True)
            d7 = C.dram_out("d_qn", [64, 512], BF16); P.d('sp', r=[q_n], out=d7[:, :], in_=q_n[:], final=True)
            d8 = C.dram_out("d_gs", [128, 6]); P.d('sp', r=[gs], out=d8[:, :], in_=gs[:, c, :], final=True)

    for c in range(NBLK):
        block(c)
    return C.finish()


_CACHE = {}

def _get(name, fn):
    if name not in _CACHE:
        _CACHE[name] = fn()
    return _CACHE[name]


def _run(nc, in_maps):
    in_maps = [{k: np.ascontiguousarray(v, dtype=np.float32) for k, v in m.items()} for m in in_maps]
    res = run_bass_kernel_spmd(nc, in_maps, core_ids=list(range(8)))
    return res.results


def kernel(x, norm_mix, norm_ffn, w_in, qk_gain, cmp_pe, cmp_w1, cmp_w2, rwkv_mu, rwkv_w0,
           rwkv_w_up, rwkv_a0, rwkv_a_up, rwkv_g_up, rwkv_k_k, rwkv_k_a, rwkv_r_k, rwkv_ln_w,
           rwkv_ln_b, vres_v0, vres_v1, vres_v2, proj_nsa, proj_rwkv, w_out, ffn_up, ffn_conv, ffn_down):
    f = lambda a: np.asarray(a, dtype=np.float32)
    x = f(x)
    B, S, Dm = x.shape
    NTOK = B * S
    TS = NTOK // 8
    CPB = 8 // B
    xT = np.ascontiguousarray(x.reshape(NTOK, Dm).T)
    NMIX = 3096
    vfirst = [None] * 8
    for l in range(4):
        nc = _get("proj", lambda: build_proj(TS, NMIX))
        Wm = f(w_in[l])[:, :NMIX]
        res = _run(nc, [{"xT": xT[:, c * TS:(c + 1) * TS], "gain": f(norm_mix[l]), "W": Wm} for c in range(8)])
        uT = np.concatenate([r["uT"] for r in res], axis=1)
        layer0 = (l == 0)
        nc = _get("rwkv%d" % int(layer0), lambda: build_rwkv(S, layer0))
        mu = f(rwkv_mu[l])
        maps = []
        for c in range(8):
            b = c // CPB; hp = c % CPB
            ub = uT[:, b * S:(b + 1) * S]
            rw = ub[1304:3096]
            own = slice(hp * 128, (hp + 1) * 128)
            pvec = np.zeros((128, NPV), np.float32)
            pvec[:, PV["mu_r"]] = mu[0:512][own]; pvec[:, PV["mu_k"]] = mu[512:1024][own]; pvec[:, PV["mu_v"]] = mu[1024:1536][own]
            pvec[:, PV["w0"]] = f(rwkv_w0[l])[own]; pvec[:, PV["a0"]] = f(rwkv_a0[l])[own]; pvec[:, PV["k_k"]] = f(rwkv_k_k[l])[own]
            pvec[:, PV["k_a"]] = f(rwkv_k_a[l])[own]; pvec[:, PV["r_k"]] = f(rwkv_r_k[l]).reshape(512)[own]
            pvec[:, PV["ln_w"]] = f(rwkv_ln_w[l])[own]; pvec[:, PV["ln_b"]] = f(rwkv_ln_b[l])[own]
            if not layer0:
                pvec[:, PV["v0"]] = f(vres_v0[l - 1])[own]
            pvec[:, PV["mu_g"]] = mu[1664:1792]; pvec[0:64, PV["mu_w"]] = mu[1536:1600]; pvec[0:64, PV["mu_a"]] = mu[1600:1664]
            for cc in range(4):
                pvec[:, PV["mu_va"] + cc] = mu[1024 + cc * 128:1024 + (cc + 1) * 128]
            m = {"zr": rw[0:512][own], "zk": rw[512:1024][own], "zvo": rw[1024:1536][own], "zw": rw[1536:1600], "za": rw[1600:1664], "zg": rw[1664:1792],
                 "pvec": pvec, "w_up": f(rwkv_w_up[l])[:, own], "a_up": f(rwkv_a_up[l])[:, own], "g_up": f(rwkv_g_up[l])[:, own]}
            if not layer0:
                m.update({"zva": rw[1024:1536], "v1": f(vres_v1[l - 1]), "v2": f(vres_v2[l - 1])[:, own], "vfirst": vfirst[c]})
            maps.append(m)
        res = _run(nc, maps)
        obT = np.zeros((512, NTOK), np.float32)
        for c in range(8):
            b = c // CPB; hp = c % CPB
            obT[hp * 128:(hp + 1) * 128, b * S:(b + 1) * S] = res[c]["oT"]
            if layer0:
                vfirst[c] = res[c]["vfo"]
        nc = _get("nsa", lambda: build_nsa(S))
        maps = []
        heads_of = []
        for c in range(8):
            b = c // CPB; g = (c % CPB) // 2; hh = c % 2
            ub = uT[:, b * S:(b + 1) * S]
            heads = [g * 4 + 2 * hh, g * 4 + 2 * hh + 1, g * 4 + 2 * (1 - hh), g * 4 + 2 * (1 - hh) + 1]
            heads_of.append(heads)
            gsl = slice(g * 64, (g + 1) * 64)
            glr = ub[1280:1304]
            m = {"qT": np.concatenate([ub[h * 64:(h + 1) * 64] for h in heads], 0),
                 "kcT": ub[512:640][gsl], "vcT": ub[640:768][gsl], "ksT": ub[768:896][gsl], "kwT": ub[1024:1152][gsl],
                 "vs": ub[896:1024][gsl].T, "vw": ub[1152:1280][gsl].T,
                 "gl": np.concatenate([glr[h * 3:(h + 1) * 3] for h in heads[:2]], 0).T,
                 "gain": f(qk_gain[l]), "pe": f(cmp_pe[l]), "w1": f(cmp_w1[l]), "w2": f(cmp_w2[l])}
            maps.append(m)
        res = _run(nc, maps)
        oaT = np.zeros((512, NTOK), np.float32)
        for c in range(8):
            b = c // CPB
            h0 = heads_of[c][0]
            oaT[h0 * 64:h0 * 64 + 128, b * S:(b + 1) * S] = res[c]["o"].T
        nc = _get("merge", lambda: build_merge(TS))
        Wg = f(w_in[l])[:, NMIX:]
        res = _run(nc, [{"xT": xT[:, c * TS:(c + 1) * TS], "gain": f(norm_mix[l]), "Wg": Wg, "oaT": oaT[:, c * TS:(c + 1) * TS], "obT": obT[:, c * TS:(c + 1) * TS],
                         "PA": f(proj_nsa[l]), "PB": f(proj_rwkv[l]), "Wo": f(w_out[l])} for c in range(8)])
        x1T = np.concatenate([r["x1T"] for r in res], axis=1)
        nc = _get("ffn", lambda: build_ffn(TS))
        maps = []
        for c in range(8):
            t0 = c * TS
            xin = np.zeros((Dm, HW + TS), np.float32)
            xin[:, HW:] = x1T[:, t0:t0 + TS]
            if t0 % S != 0:
                xin[:, :HW] = x1T[:, t0 - HW:t0]
            maps.append({"xT": xin, "gain": f(norm_ffn[l]), "Wup": f(ffn_up[l]), "cw": f(ffn_conv[l]), "Wd": f(ffn_down[l])})
        res = _run(nc, maps)
        xT = np.concatenate([r["x2T"] for r in res], axis=1)
    return np.ascontiguousarray(xT.T).reshape(B, S, Dm).astype(np.float32)
```

```python
import contextlib
import math
import numpy as np
import concourse.bass as bass
import concourse.mybir as mybir
from concourse.bass_utils import run_bass_kernel_spmd

F32 = mybir.dt.float32
BF16 = mybir.dt.bfloat16
AF = mybir.ActivationFunctionType
ALU = mybir.AluOpType
AX = mybir.AxisListType


class Prog:
    COMPUTE = ('pe', 'act', 'dve', 'pool')
    NSLOT = 6

    def __init__(self, nc, same_engine_sync=True):
        self.nc = nc
        self.streams = {e: [] for e in ('pe', 'act', 'dve', 'pool', 'sp')}
        self.count = {e: 0 for e in self.streams}
        self.waited = {e: {} for e in self.streams}
        self.last_w = {}
        self.readers = {}
        self.dma_slot = {q: 0 for q in ('sp', 'act', 'pool')}
        self.dma_cnt = {}
        self.same_engine_sync = same_engine_sync
        self.out_dma = []
        self.targets = set()

    def _need(self, eng, dep):
        semk, val = dep
        if semk == eng and (eng == 'pe' or not self.same_engine_sync):
            return
        w = self.waited[eng]
        if w.get(semk, 0) >= val:
            return
        w[semk] = val
        self.targets.add((semk, val))
        self.streams[eng].append(('wait', semk, val))

    @staticmethod
    def _k(k):
        if isinstance(k, (str, tuple, int)):
            return k
        return 'T:' + str(k.name)

    def _deps(self, eng, r, w):
        r = [self._k(k) for k in r]
        w = [self._k(k) for k in w]
        for k in r:
            if k in self.last_w:
                self._need(eng, self.last_w[k])
        for k in w:
            if k in self.last_w:
                self._need(eng, self.last_w[k])
            for d in self.readers.get(k, ()):
                self._need(eng, d)

    def _record(self, tag, r, w):
        r = [self._k(k) for k in r]
        w = [self._k(k) for k in w]
        for k in r:
            self.readers.setdefault(k, []).append(tag)
        for k in w:
            self.last_w[k] = tag
            self.readers[k] = []

    def op(self, eng, fn, r=(), w=()):
        self._deps(eng, r, w)
        self.count[eng] += 1
        tag = (eng, self.count[eng])
        self.streams[eng].append(('inst', fn, eng, 1, self.count[eng]))
        self._record(tag, r, w)

    def i(self, eng, name, r=(), w=(), **kw):
        self.op(eng, lambda e: getattr(e, name)(**kw), r=r, w=w)

    def d(self, q, r=(), w=(), final=False, **kw):
        self.dma(q, lambda e: e.dma_start(**kw), r=r, w=w, final=final)

    def dma(self, q, fn, r=(), w=(), final=False):
        s = self.dma_slot[q]
        self.dma_slot[q] = (s + 1) % self.NSLOT
        semk = ('dma', q, s)
        prev = self.dma_cnt.get(semk, 0)
        if prev:
            self._need(q, (semk, prev))
        self._deps(q, r, w)
        self.dma_cnt[semk] = prev + 16
        tag = (semk, prev + 16)
        self.streams[q].append(('inst', fn, semk, 16))
        self._record(tag, r, w)
        if final:
            self.out_dma.append(tag)

    def emit(self):
        nc = self.nc
        semkeys = list(self.COMPUTE) + [('dma', q, s) for q in ('sp', 'act', 'pool') for s in range(self.NSLOT)]
        for tag in self.out_dma:
            self._need('sp', tag)
        import contextlib
        with contextlib.ExitStack() as st:
            sems = {}
            for i, k in enumerate(semkeys):
                sems[k] = st.enter_context(nc.semaphore("s%d" % i))
            block = st.enter_context(nc.Block())

            import bisect
            sig = {e: sorted(v for (k, v) in self.targets if k == e) for e in self.COMPUTE}

            def replay(name):
                def f(e):
                    for it in self.streams[name]:
                        if it[0] == 'wait':
                            if it[1] in sig:
                                e.wait_ge(sems[it[1]], bisect.bisect_right(sig[it[1]], it[2]))
                            else:
                                e.wait_ge(sems[it[1]], it[2])
                        elif it[2] in sig:
                            ins = it[1](e)
                            if (it[2], it[4]) in self.targets:
                                ins.then_inc(sems[it[2]], 1)
                        else:
                            it[1](e).then_inc(sems[it[2]], it[3])
                return f
            block.sync(replay('sp'))
            block.scalar(replay('act'))
            block.vector(replay('dve'))
            block.gpsimd(replay('pool'))
            block.tensor(replay('pe'))


EPS = 1e-6
D = 1024

class Ctx:
    def __init__(self):
        self.nc = bass.Bass("TRN2", target_bir_lowering=False)
        self.P = Prog(self.nc)
        self.st = contextlib.ExitStack()
        self.n = 0
        self.psn = 0
        self.rr = 0
    def dram_in(self, name, shape, dt=F32):
        return self.nc.dram_tensor(name, list(shape), dt, kind="ExternalInput").ap()
    def dram_out(self, name, shape, dt=F32):
        return self.nc.dram_tensor(name, list(shape), dt, kind="ExternalOutput").ap()
    def dram_tmp(self, name, shape, dt=F32):
        return self.nc.dram_tensor(name, list(shape), dt, kind="Internal").ap()
    def sb(self, name, shape, dt=F32):
        return self.st.enter_context(self.nc.sbuf_tensor(name, list(shape), dt))
    def ps(self, name, shape, dt=F32):
        return self.st.enter_context(self.nc.psum_tensor(name, list(shape), dt))
    def psbanks(self, n=8):
        self.banks = [self.ps("psb%d" % i, [128, 512]) for i in range(n)]
        return self.banks
    def bank(self):
        b = self.banks[self.psn % len(self.banks)]
        self.psn += 1
        return b
    def finish(self):
        self.P.emit()
        self.st.close()
        return self.nc

    def load_w_bf16(self, name, w_ap, K, N, SC=2048):
        P = self.P
        KC = K // 128
        wb = self.sb(name, [128, KC, N], BF16)
        if not hasattr(self, 'stage'):
            self.stage = [self.sb("wstage%d" % i, [128, SC]) for i in range(2)]
        i = self.rr
        for c in range(KC):
            for n0 in range(0, N, SC):
                n1 = min(N, n0 + SC)
                sgt = self.stage[i % 2]
                P.dma('sp', (lambda sgt, c, n0, n1: lambda e: e.dma_start(out=sgt[:, 0:n1 - n0], in_=w_ap[c * 128:(c + 1) * 128, n0:n1]))(sgt, c, n0, n1), w=[sgt])
                eng = 'dve' if i % 2 == 0 else 'pool'
                P.op(eng, (lambda sgt, c, n0, n1: lambda e: e.tensor_copy(out=wb[:, c, n0:n1], in_=sgt[:, 0:n1 - n0]))(sgt, c, n0, n1), r=[sgt], w=[wb])
                i += 1
        self.rr = i
        return wb

    def load_vec_t(self, name, v_ap, n):
        t = self.sb(name, [128, n // 128])
        self.P.dma('sp', lambda e: e.dma_start(out=t[:], in_=v_ap.rearrange("(c p) -> p c", p=128), allow_slow_non_contiguous=True), w=[t])
        return t

    def consts(self):
        P = self.P
        self.ones_bf = self.sb("ones_bf", [128, 128], BF16)
        P.op('pool', lambda e: e.memset(self.ones_bf[:], 1.0), w=[self.ones_bf])

    def rmsnorm(self, xt, gain_t, w, hT, sq, rstd):
        P = self.P
        for c in range(8):
            P.op('act', (lambda c: lambda e: e.activation(out=sq[:, c, 0:w], in_=xt[:, c, 0:w], func=AF.Square))(c), r=[xt], w=[sq])
        pb = self.bank()
        for c in range(8):
            P.op('pe', (lambda c: lambda e: e.matmul(pb[:, 0:w], lhsT=self.ones_bf[:], rhs=sq[:, c, 0:w], start=(c == 0), stop=(c == 7)))(c), r=[sq, self.ones_bf], w=[pb])
        P.op('dve', lambda e: e.tensor_scalar(out=rstd[:, 0:w], in0=pb[:, 0:w], scalar1=1.0 / D, scalar2=EPS, op0=ALU.mult, op1=ALU.add), r=[pb], w=[rstd])
        P.op('act', lambda e: e.activation(out=rstd[:, 0:w], in_=rstd[:, 0:w], func=AF.Sqrt), r=[rstd], w=[rstd])
        P.op('dve', lambda e: e.reciprocal(out=rstd[:, 0:w], in_=rstd[:, 0:w]), r=[rstd], w=[rstd])
        for c in range(8):
            P.op('dve', (lambda c: lambda e: e.scalar_tensor_tensor(out=hT[:, c, 0:w], in0=xt[:, c, 0:w], scalar=gain_t[:, c:c + 1], in1=rstd[:, 0:w], op0=ALU.mult, op1=ALU.mult))(c), r=[xt, gain_t, rstd], w=[hT])


def build_proj(T, NCOL):
    C = Ctx(); P = C.P
    xT = C.dram_in("xT", [D, T]); gain = C.dram_in("gain", [D]); W = C.dram_in("W", [D, NCOL])
    uT = C.dram_out("uT", [NCOL, T])
    C.consts(); C.psbanks()
    gain_t = C.load_vec_t("gain_t", gain, D)
    Wb = C.load_w_bf16("Wb", W, D, NCOL)
    TW = 512
    xts = [C.sb("xt%d" % i, [128, 8, TW]) for i in range(2)]
    sq = C.sb("sq", [128, 8, TW], BF16); rstd = C.sb("rstd", [128, TW])
    hTs = [C.sb("hT%d" % i, [128, 8, TW], BF16) for i in range(2)]
    obs = [C.sb("ob%d" % i, [128, TW]) for i in range(4)]
    xv = xT.rearrange("(c p) t -> p c t", p=128)
    oi = 0
    for ti, t0 in enumerate(range(0, T, TW)):
        xt = xts[ti % 2]; hT = hTs[ti % 2]
        P.dma('sp', (lambda xt, t0: lambda e: e.dma_start(out=xt[:], in_=xv[:, :, t0:t0 + TW]))(xt, t0), w=[xt])
        C.rmsnorm(xt, gain_t, TW, hT, sq, rstd)
        for j0 in range(0, NCOL, 128):
            m = min(128, NCOL - j0)
            pb = C.bank()
            for c in range(8):
                P.op('pe', (lambda c, j0, m, pb, hT: lambda e: e.matmul(pb[0:m, :], lhsT=Wb[:, c, j0:j0 + m], rhs=hT[:, c, :], start=(c == 0), stop=(c == 7)))(c, j0, m, pb, hT), r=[Wb, hT], w=[pb])
            ob = obs[oi % 4]
            eng = 'act' if oi % 2 == 0 else 'dve'
            if eng == 'act':
                P.op('act', (lambda ob, pb, m: lambda e: e.activation(out=ob[0:m, :], in_=pb[0:m, :], func=AF.Copy))(ob, pb, m), r=[pb], w=[ob])
            else:
                P.op('dve', (lambda ob, pb, m: lambda e: e.tensor_copy(out=ob[0:m, :], in_=pb[0:m, :]))(ob, pb, m), r=[pb], w=[ob])
            P.dma('pool', (lambda ob, j0, m, t0: lambda e: e.dma_start(out=uT[j0:j0 + m, t0:t0 + TW], in_=ob[0:m, :]))(ob, j0, m, t0), r=[ob], final=True)
            oi += 1
    return C.finish()


def build_merge(T):
    C = Ctx(); P = C.P
    xT = C.dram_in("xT", [D, T]); gain = C.dram_in("gain", [D]); Wg = C.dram_in("Wg", [D, 2 * D])
    oaT = C.dram_in("oaT", [512, T]); obT = C.dram_in("obT", [512, T])
    PA = C.dram_in("PA", [512, D]); PB = C.dram_in("PB", [512, D]); Wo = C.dram_in("Wo", [D, D])
    x1T = C.dram_out("x1T", [D, T])
    C.consts(); C.psbanks()
    gain_t = C.load_vec_t("gain_t", gain, D)
    Wgb = C.load_w_bf16("Wgb", Wg, D, 2 * D)
    PAb = C.load_w_bf16("PAb", PA, 512, D)
    PBb = C.load_w_bf16("PBb", PB, 512, D)
    Wob = C.load_w_bf16("Wob", Wo, D, D)
    TW = 512
    xt = C.sb("xt", [128, 8, TW]); sq = C.sb("sq", [128, 8, TW], BF16); rstd = C.sb("rstd", [128, TW])
    hT = C.sb("hT", [128, 8, TW], BF16)
    oaf = C.sb("oaf", [128, 4, TW]); obf = C.sb("obf", [128, 4, TW])
    oab = C.sb("oab", [128, 4, TW], BF16); obb = C.sb("obb", [128, 4, TW], BF16)
    yb = C.sb("yb", [128, 8, TW], BF16)
    sga = [C.sb("sga%d" % i, [128, TW]) for i in range(2)]; sgb = [C.sb("sgb%d" % i, [128, TW]) for i in range(2)]
    t1 = [C.sb("t1_%d" % i, [128, TW]) for i in range(2)]; t2 = [C.sb("t2_%d" % i, [128, TW]) for i in range(2)]
    obs = [C.sb("ob%d" % i, [128, TW]) for i in range(2)]
    xv = xT.rearrange("(c p) t -> p c t", p=128)
    oav = oaT.rearrange("(c p) t -> p c t", p=128); obv = obT.rearrange("(c p) t -> p c t", p=128)
    for ti, t0 in enumerate(range(0, T, TW)):
        P.dma('sp', (lambda t0: lambda e: e.dma_start(out=xt[:], in_=xv[:, :, t0:t0 + TW]))(t0), w=[xt])
        P.dma('sp', (lambda t0: lambda e: e.dma_start(out=oaf[:], in_=oav[:, :, t0:t0 + TW]))(t0), w=[oaf])
        P.dma('sp', (lambda t0: lambda e: e.dma_start(out=obf[:], in_=obv[:, :, t0:t0 + TW]))(t0), w=[obf])
        P.op('pool', lambda e: e.tensor_copy(out=oab[:], in_=oaf[:]), r=[oaf], w=[oab])
        P.op('pool', lambda e: e.tensor_copy(out=obb[:], in_=obf[:]), r=[obf], w=[obb])
        C.rmsnorm(xt, gain_t, TW, hT, sq, rstd)
        for d in range(8):
            ds = slice(d * 128, (d + 1) * 128)
            pya = C.bank(); pga = C.bank(); pyb = C.bank(); pgb = C.bank()
            for k in range(4):
                P.op('pe', (lambda k, ds, pb: lambda e: e.matmul(pb[:], lhsT=PAb[:, k, ds], rhs=oab[:, k, :], start=(k == 0), stop=(k == 3)))(k, ds, pya), r=[PAb, oab], w=[pya])
            for k in range(8):
                P.op('pe', (lambda k, ds, pb: lambda e: e.matmul(pb[:], lhsT=Wgb[:, k, ds], rhs=hT[:, k, :], start=(k == 0), stop=(k == 7)))(k, ds, pga), r=[Wgb, hT], w=[pga])
            for k in range(4):
                P.op('pe', (lambda k, ds, pb: lambda e: e.matmul(pb[:], lhsT=PBb[:, k, ds], rhs=obb[:, k, :], start=(k == 0), stop=(k == 3)))(k, ds, pyb), r=[PBb, obb], w=[pyb])
            ds2 = slice(D + d * 128, D + (d + 1) * 128)
            for k in range(8):
                P.op('pe', (lambda k, ds2, pb: lambda e: e.matmul(pb[:], lhsT=Wgb[:, k, ds2], rhs=hT[:, k, :], start=(k == 0), stop=(k == 7)))(k, ds2, pgb), r=[Wgb, hT], w=[pgb])
            sa = sga[d % 2]; sb_ = sgb[d % 2]; ta = t1[d % 2]; tb = t2[d % 2]
            P.op('act', (lambda sa, pga: lambda e: e.activation(out=sa[:], in_=pga[:], func=AF.Sigmoid))(sa, pga), r=[pga], w=[sa])
            P.op('act', (lambda sb_, pgb: lambda e: e.activation(out=sb_[:], in_=pgb[:], func=AF.Sigmoid))(sb_, pgb), r=[pgb], w=[sb_])
            P.op('dve', (lambda ta, sa, pya: lambda e: e.tensor_tensor(out=ta[:], in0=pya[:], in1=sa[:], op=ALU.mult))(ta, sa, pya), r=[pya, sa], w=[ta])
            P.op('dve', (lambda tb, sb_, pyb: lambda e: e.tensor_tensor(out=tb[:], in0=pyb[:], in1=sb_[:], op=ALU.mult))(tb, sb_, pyb), r=[pyb, sb_], w=[tb])
            P.op('pool', (lambda d, ta, tb: lambda e: e.tensor_tensor(out=yb[:, d, :], in0=ta[:], in1=tb[:], op=ALU.add))(d, ta, tb), r=[ta, tb], w=[yb])
        for d in range(8):
            ds = slice(d * 128, (d + 1) * 128)
            pb = C.bank()
            for k in range(8):
                P.op('pe', (lambda k, ds, pb: lambda e: e.matmul(pb[:], lhsT=Wob[:, k, ds], rhs=yb[:, k, :], start=(k == 0), stop=(k == 7)))(k, ds, pb), r=[Wob, yb], w=[pb])
            ob = obs[d % 2]
            P.op('dve', (lambda d, ob, pb: lambda e: e.tensor_tensor(out=ob[:], in0=pb[:], in1=xt[:, d, :], op=ALU.add))(d, ob, pb), r=[pb, xt], w=[ob])
            P.dma('pool', (lambda ob, d, t0: lambda e: e.dma_start(out=x1T[d * 128:(d + 1) * 128, t0:t0 + TW], in_=ob[:]))(ob, d, t0), r=[ob], final=True)
    return C.finish()


DFF = 2816
HW = 32
def build_ffn(T, HALO=True):
    C = Ctx(); P = C.P
    xT = C.dram_in("xT", [D, T + HW]); gain = C.dram_in("gain", [D]); Wup = C.dram_in("Wup", [D, 2 * DFF])
    cw = C.dram_in("cw", [3, 2 * DFF]); Wd = C.dram_in("Wd", [DFF, D])
    x2T = C.dram_out("x2T", [D, T])
    C.consts(); C.psbanks()
    gain_t = C.load_vec_t("gain_t", gain, D)
    cwt = C.sb("cwt", [128, 3, 44])
    for j in range(3):
        P.dma('sp', (lambda j: lambda e: e.dma_start(out=cwt[:, j, :], in_=cw[j, :].rearrange("(c p) -> p c", p=128), allow_slow_non_contiguous=True))(j), w=[cwt])
    Wub = C.load_w_bf16("Wub", Wup, D, 2 * DFF, SC=1408)
    Wdb = C.load_w_bf16("Wdb", Wd, DFF, D, SC=1408)
    TW = 256
    xt = C.sb("xt", [128, 8, TW]); sq = C.sb("sq", [128, 8, TW], BF16); rstd = C.sb("rstd", [128, TW])
    hT = C.sb("hT", [128, 8, TW], BF16)
    gb = C.sb("gb", [128, 22, TW], BF16)
    sta = C.sb("sta", [128, 22, 2]); stb = C.sb("stb", [128, 22, 2])
    P.op('pool', lambda e: e.memset(sta[:], 0.0), w=[sta])
    P.op('pool', lambda e: e.memset(stb[:], 0.0), w=[stb])
    ua = [C.sb("ua%d" % i, [128, TW + 2]) for i in range(2)]; ub = [C.sb("ub%d" % i, [128, TW + 2]) for i in range(2)]
    ca = [C.sb("ca%d" % i, [128, TW]) for i in range(2)]; cb = [C.sb("cb%d" % i, [128, TW]) for i in range(2)]
    sa = [C.sb("sa%d" % i, [128, TW]) for i in range(2)]
    obs = [C.sb("ob%d" % i, [128, TW]) for i in range(2)]
    xv = xT.rearrange("(c p) t -> p c t", p=128)
    tiles = ([(0, HW)] if HALO else [(0, 0)]) + [(HW + t0, TW) for t0 in range(0, T, TW)]
    def do_tile(ti, c0, w):
        P.dma('sp', (lambda c0, w: lambda e: e.dma_start(out=xt[:, :, 0:w], in_=xv[:, :, c0:c0 + w]))(c0, w), w=[xt])
        C.rmsnorm(xt, gain_t, w, hT, sq, rstd)
        for i in range(22):
            pa = C.bank(); pb = C.bank()
            for k in range(8):
                P.op('pe', (lambda k, i, pa: lambda e: e.matmul(pa[:, 0:w], lhsT=Wub[:, k, i * 128:(i + 1) * 128], rhs=hT[:, k, 0:w], start=(k == 0), stop=(k == 7)))(k, i, pa), r=[Wub, hT], w=[pa])
            for k in range(8):
                P.op('pe', (lambda k, i, pb: lambda e: e.matmul(pb[:, 0:w], lhsT=Wub[:, k, DFF + i * 128:DFF + (i + 1) * 128], rhs=hT[:, k, 0:w], start=(k == 0), stop=(k == 7)))(k, i, pb), r=[Wub, hT], w=[pb])
            for (pp, uu, stt, cc, fo, eng) in ((pa, ua[i % 2], sta, ca[i % 2], i, 'dve'), (pb, ub[i % 2], stb, cb[i % 2], 22 + i, 'dve')):
                P.op('act', (lambda pp, uu: lambda e: e.activation(out=uu[:, 2:2 + w], in_=pp[:, 0:w], func=AF.Copy))(pp, uu), r=[pp], w=[uu])
                P.op(eng, (lambda uu, stt, i: lambda e: e.tensor_copy(out=uu[:, 0:2], in_=stt[:, i, :]))(uu, stt, i), r=[stt], w=[uu])
                P.op(eng, (lambda uu, stt, i: lambda e: e.tensor_copy(out=stt[:, i, :], in_=uu[:, w:w + 2]))(uu, stt, i), r=[uu], w=[stt])
                P.op(eng, (lambda uu, cc, fo: lambda e: e.tensor_scalar(out=cc[:, 0:w], in0=uu[:, 0:w], scalar1=cwt[:, 0, fo:fo + 1], scalar2=None, op0=ALU.mult))(uu, cc, fo), r=[uu, cwt], w=[cc])
                P.op(eng, (lambda uu, cc, fo: lambda e: e.scalar_tensor_tensor(out=cc[:, 0:w], in0=uu[:, 1:1 + w], scalar=cwt[:, 1, fo:fo + 1], in1=cc[:, 0:w], op0=ALU.mult, op1=ALU.add))(uu, cc, fo), r=[uu, cwt, cc], w=[cc])
                P.op(eng, (lambda uu, cc, fo: lambda e: e.scalar_tensor_tensor(out=cc[:, 0:w], in0=uu[:, 2:2 + w], scalar=cwt[:, 2, fo:fo + 1], in1=cc[:, 0:w], op0=ALU.mult, op1=ALU.add))(uu, cc, fo), r=[uu, cwt, cc], w=[cc])
            if ti == 0:
                continue
            s_ = sa[i % 2]
            P.op('act', (lambda s_, cc: lambda e: e.activation(out=s_[:, 0:w], in_=cc[:, 0:w], func=AF.Silu))(s_, ca[i % 2]), r=[ca[i % 2]], w=[s_])
            P.op('dve', (lambda s_, cc, i: lambda e: e.tensor_tensor(out=gb[:, i, 0:w], in0=s_[:, 0:w], in1=cc[:, 0:w], op=ALU.mult))(s_, cb[i % 2], i), r=[s_, cb[i % 2]], w=[gb])
        if ti == 0:
            return
        for d in range(8):
            pb = C.bank()
            for i in range(22):
                P.op('pe', (lambda i, d, pb: lambda e: e.matmul(pb[:, 0:w], lhsT=Wdb[:, i, d * 128:(d + 1) * 128], rhs=gb[:, i, 0:w], start=(i == 0), stop=(i == 21)))(i, d, pb), r=[Wdb, gb], w=[pb])
            ob = obs[d % 2]
            P.op('dve', (lambda d, ob, pb: lambda e: e.tensor_tensor(out=ob[:, 0:w], in0=pb[:, 0:w], in1=xt[:, d, 0:w], op=ALU.add))(d, ob, pb), r=[pb, xt], w=[ob])
            P.dma('pool', (lambda ob, d, c0: lambda e: e.dma_start(out=x2T[d * 128:(d + 1) * 128, c0 - HW:c0 - HW + w], in_=ob[:, 0:w]))(ob, d, c0), r=[ob], final=True)
    for ti, (c0, w) in enumerate(tiles):
        if w > 0:
            do_tile(ti, c0, w)
    return C.finish()


LNX_EPS = 1e-5 * 64
PV = dict(mu_r=0, mu_k=1, mu_v=2, w0=3, a0=4, k_k=5, k_a=6, r_k=7, ln_w=8, ln_b=9, v0=10, mu_g=11, mu_w=12, mu_a=13, mu_va=14)
NPV = 18


def build_rwkv(T, layer0, same_sync=False, G=8, SPLIT=False):
    C = Ctx(); P = C.P
    P.same_engine_sync = same_sync
    zr = C.dram_in("zr", [128, T]); zk = C.dram_in("zk", [128, T]); zvo = C.dram_in("zvo", [128, T])
    zw = C.dram_in("zw", [64, T]); za = C.dram_in("za", [64, T]); zg = C.dram_in("zg", [128, T])
    pvec = C.dram_in("pvec", [128, NPV])
    w_up = C.dram_in("w_up", [64, 128]); a_up = C.dram_in("a_up", [64, 128]); g_up = C.dram_in("g_up", [128, 128])
    if not layer0:
        zva = C.dram_in("zva", [512, T]); v1 = C.dram_in("v1", [512, 32]); v2 = C.dram_in("v2", [32, 128]); vfirst = C.dram_in("vfirst", [128, T])
    oT = C.dram_out("oT", [128, T])
    if layer0:
        vfo = C.dram_out("vfo", [128, T])
    SCR = {n: C.dram_tmp("scr_" + n, [128, T]) for n in ("W", "A", "B", "K", "R", "V", "G", "BV")}
    C.psbanks()
    pv = C.sb("pv", [128, NPV]); P.d('sp', w=[pv], out=pv[:], in_=pvec[:, :])
    wup_t = C.sb("wup_t", [64, 128]); P.d('sp', w=[wup_t], out=wup_t[:], in_=w_up[:, :])
    aup_t = C.sb("aup_t", [64, 128]); P.d('sp', w=[aup_t], out=aup_t[:], in_=a_up[:, :])
    gup_t = C.sb("gup_t", [128, 128]); P.d('sp', w=[gup_t], out=gup_t[:], in_=g_up[:, :])
    if not layer0:
        v1_t = C.sb("v1_t", [128, 4, 32]); P.d('sp', w=[v1_t], out=v1_t[:], in_=v1.rearrange("(c p) r -> p c r", p=128))
        v2_t = C.sb("v2_t", [32, 128]); P.d('sp', w=[v2_t], out=v2_t[:], in_=v2[:, :])
    I2 = C.sb("I2", [128, 64]); bo = C.sb("bo", [128, 128])
    P.i('pool', 'memset', w=[I2], ap=I2[:], constant=1.0)
    for h in range(2):
        P.i('pool', 'affine_select', r=[I2], w=[I2], out=I2[h * 64:(h + 1) * 64, :], in_=I2[h * 64:(h + 1) * 64, :], pattern=[[-1, 64]], compare_op=ALU.is_equal, fill=0.0, base=0, channel_multiplier=1)
    P.i('pool', 'memset', w=[bo], ap=bo[:], constant=0.0)
    P.i('pool', 'memset', w=[bo], ap=bo[0:64, 0:64], constant=1.0)
    P.i('pool', 'memset', w=[bo], ap=bo[64:128, 64:128], constant=1.0)

    def col(n, rows=128, off=0):
        return pv[0:rows, PV[n] + off:PV[n] + off + 1]

    TP = 512
    NB = 2
    zt = {n: [C.sb("z_%s%d" % (n, i), [rows, TP + 1]) for i in range(NB)] for n, rows in (("r", 128), ("k", 128), ("vo", 128), ("w", 64), ("a", 64), ("g", 128))}
    zs = {n: C.sb("zs_" + n, [rows, TP]) for n, rows in (("r", 128), ("k", 128), ("vo", 128), ("w", 64), ("a", 64), ("g", 128))}
    if not layer0:
        zva_t = [C.sb("zva_t%d" % i, [128, 4, TP + 1]) for i in range(NB)]
        zva_s = C.sb("zva_s", [128, 4, TP])
        vf_t = [C.sb("vf_t%d" % i, [128, TP]) for i in range(NB)]
        lr = C.sb("lr", [32, TP]); vm = C.sb("vm", [128, TP])
    tmp = C.sb("tmp", [128, TP]); tmp2 = C.sb("tmp2", [128, TP])
    wdec = C.sb("wdec", [128, TP]); aa = C.sb("aa", [128, TP]); gg = C.sb("gg", [128, TP]); vv = C.sb("vv", [128, TP])
    kk = C.sb("kk", [128, TP]); kn = C.sb("kn", [128, TP]); at_ = C.sb("at_", [128, TP]); bt_ = C.sb("bt_", [128, TP]); kp = C.sb("kp", [128, TP]); bv = C.sb("bv", [128, TP])
    srcs = dict(r=zr, k=zk, vo=zvo, w=zw, a=za, g=zg)
    mus = dict(r=("mu_r", 128), k=("mu_k", 128), vo=("mu_v", 128), w=("mu_w", 64), a=("mu_a", 64), g=("mu_g", 128))

    def prep_tile(ti, t0):
        par = ti % NB
        for n, src in srcs.items():
            z = zt[n][par]; rows = mus[n][1]
            if t0 == 0:
                P.i('pool', 'memset', w=[z], ap=z[:, 0:1], constant=0.0)
                P.d('sp', w=[z], out=z[:, 1:TP + 1], in_=src[:, 0:TP])
            else:
                P.d('sp', w=[z], out=z[:], in_=src[:, t0 - 1:t0 + TP])
            P.i('dve', 'tensor_tensor', r=[z], w=[tmp], out=tmp[0:rows, :], in0=z[:, 0:TP], in1=z[:, 1:TP + 1], op=ALU.subtract)
            P.i('dve', 'scalar_tensor_tensor', r=[tmp, z, pv], w=[zs[n]], out=zs[n][:], in0=tmp[0:rows, :], scalar=col(mus[n][0], rows), in1=z[:, 1:TP + 1], op0=ALU.mult, op1=ALU.add)
        if not layer0:
            z = zva_t[par]
            if t0 == 0:
                P.i('pool', 'memset', w=[z], ap=z[:, :, 0:1], constant=0.0)
                P.d('sp', w=[z], out=z[:, :, 1:TP + 1], in_=zva.rearrange("(c p) t -> p c t", p=128)[:, :, 0:TP])
            else:
                P.d('sp', w=[z], out=z[:], in_=zva.rearrange("(c p) t -> p c t", p=128)[:, :, t0 - 1:t0 + TP])
            P.d('sp', w=[vf_t[par]], out=vf_t[par][:], in_=vfirst[:, t0:t0 + TP])
            for c in range(4):
                P.i('dve', 'tensor_tensor', r=[z], w=[tmp], out=tmp[:], in0=z[:, c, 0:TP], in1=z[:, c, 1:TP + 1], op=ALU.subtract)
                P.i('dve', 'scalar_tensor_tensor', r=[tmp, z, pv], w=[zva_s], out=zva_s[:, c, :], in0=tmp[:], scalar=col("mu_va", 128, c), in1=z[:, c, 1:TP + 1], op0=ALU.mult, op1=ALU.add)
        sl = slice(t0, t0 + TP)
        P.i('act', 'activation', r=[zs["w"]], w=[tmp2], out=tmp2[0:64, :], in_=zs["w"][:], func=AF.Tanh)
        pb = C.bank()
        P.i('pe', 'matmul', r=[wup_t, tmp2], w=[pb], out=pb[:], lhsT=wup_t[:], rhs=tmp2[0:64, :], start=True, stop=True)
        P.i('act', 'activation', r=[pb, pv], w=[wdec], out=wdec[:], in_=pb[:], func=AF.Sigmoid, bias=col("w0"))
        P.i('act', 'activation', r=[wdec], w=[wdec], out=wdec[:], in_=wdec[:], func=AF.Exp, scale=-math.exp(-0.5))
        P.d('pool', r=[wdec], w=[SCR["W"]], out=SCR["W"][:, sl], in_=wdec[:])
        pb = C.bank()
        P.i('pe', 'matmul', r=[aup_t, zs["a"]], w=[pb], out=pb[:], lhsT=aup_t[:], rhs=zs["a"][:], start=True, stop=True)
        P.i('act', 'activation', r=[pb, pv], w=[aa], out=aa[:], in_=pb[:], func=AF.Sigmoid, bias=col("a0"))
        P.i('act', 'activation', r=[zs["g"]], w=[tmp2], out=tmp2[:], in_=zs["g"][:], func=AF.Sigmoid)
        pb = C.bank()
        P.i('pe', 'matmul', r=[gup_t, tmp2], w=[pb], out=pb[:], lhsT=gup_t[:], rhs=tmp2[:], start=True, stop=True)
        P.i('act', 'activation', r=[pb], w=[gg], out=gg[:], in_=pb[:], func=AF.Copy)
        P.d('pool', r=[gg], w=[SCR["G"]], out=SCR["G"][:, sl], in_=gg[:])
        if layer0:
            P.i('pool', 'tensor_copy', r=[zs["vo"]], w=[vv], out=vv[:], in_=zs["vo"][:])
            P.d('pool', r=[vv], out=vfo[:, sl], in_=vv[:], final=True)
        else:
            pb = C.bank()
            for c in range(4):
                P.i('pe', 'matmul', r=[v1_t, zva_s], w=[pb], out=pb[0:32, :], lhsT=v1_t[:, c, :], rhs=zva_s[:, c, :], start=(c == 0), stop=(c == 3))
            P.i('act', 'activation', r=[pb], w=[lr], out=lr[:], in_=pb[0:32, :], func=AF.Copy)
            pb2 = C.bank()
            P.i('pe', 'matmul', r=[v2_t, lr], w=[pb2], out=pb2[:], lhsT=v2_t[:], rhs=lr[:], start=True, stop=True)
            P.i('act', 'activation', r=[pb2, pv], w=[vm], out=vm[:], in_=pb2[:], func=AF.Sigmoid, bias=col("v0"))
            P.i('dve', 'tensor_tensor', r=[vf_t[par], zs["vo"]], w=[tmp], out=tmp[:], in0=vf_t[par][:], in1=zs["vo"][:], op=ALU.subtract)
            P.i('dve', 'tensor_tensor', r=[tmp, vm], w=[tmp], out=tmp[:], in0=tmp[:], in1=vm[:], op=ALU.mult)
            P.i('dve', 'tensor_tensor', r=[tmp, zs["vo"]], w=[vv], out=vv[:], in0=tmp[:], in1=zs["vo"][:], op=ALU.add)
        P.d('pool', r=[vv], w=[SCR["V"]], out=SCR["V"][:, sl], in_=vv[:])
        P.i('dve', 'tensor_scalar', r=[zs["k"], pv], w=[kk], out=kk[:], in0=zs["k"][:], scalar1=col("k_k"), scalar2=None, op0=ALU.mult)
        P.i('act', 'activation', r=[kk], w=[tmp2], out=tmp2[:], in_=kk[:], func=AF.Square)
        pb = C.bank()
        P.i('pe', 'matmul', r=[bo, tmp2], w=[pb], out=pb[:], lhsT=bo[:], rhs=tmp2[:], start=True, stop=True)
        P.i('act', 'activation', r=[pb], w=[kn], out=kn[:], in_=pb[:], func=AF.Sqrt)
        P.i('dve', 'tensor_scalar', r=[kn], w=[kn], out=kn[:], in0=kn[:], scalar1=1e-12, scalar2=None, op0=ALU.max)
        P.i('dve', 'reciprocal', r=[kn], w=[kn], out=kn[:], in_=kn[:])
        P.i('dve', 'tensor_tensor', r=[kk, kn], w=[kn], out=kn[:], in0=kk[:], in1=kn[:], op=ALU.mult)
        P.i('dve', 'tensor_scalar', r=[kn], w=[at_], out=at_[:], in0=kn[:], scalar1=-1.0, scalar2=None, op0=ALU.mult)
        P.d('pool', r=[at_], w=[SCR["A"]], out=SCR["A"][:, sl], in_=at_[:])
        P.i('dve', 'tensor_tensor', r=[kn, aa], w=[bt_], out=bt_[:], in0=kn[:], in1=aa[:], op=ALU.mult)
        P.d('pool', r=[bt_], w=[SCR["B"]], out=SCR["B"][:, sl], in_=bt_[:])
        P.i('dve', 'tensor_scalar', r=[aa, pv], w=[tmp], out=tmp[:], in0=aa[:], scalar1=-1.0, scalar2=col("k_a"), op0=ALU.add, op1=ALU.mult)
        P.i('dve', 'scalar_tensor_tensor', r=[tmp, zs["k"]], w=[kp], out=kp[:], in0=tmp[:], scalar=1.0, in1=zs["k"][:], op0=ALU.add, op1=ALU.mult)
        P.d('pool', r=[kp], w=[SCR["K"]], out=SCR["K"][:, sl], in_=kp[:])
        P.d('pool', r=[zs["r"]], w=[SCR["R"]], out=SCR["R"][:, sl], in_=zs["r"][:])
        P.i('dve', 'scalar_tensor_tensor', r=[zs["r"], kp, pv], w=[tmp2], out=tmp2[:], in0=zs["r"][:], scalar=col("r_k"), in1=kp[:], op0=ALU.mult, op1=ALU.mult)
        pb = C.bank()
        P.i('pe', 'matmul', r=[bo, tmp2], w=[pb], out=pb[:], lhsT=bo[:], rhs=tmp2[:], start=True, stop=True)
        P.i('dve', 'tensor_tensor', r=[pb, vv], w=[bv], out=bv[:], in0=pb[:], in1=vv[:], op=ALU.mult)
        P.d('pool', r=[bv], w=[SCR["BV"]], out=SCR["BV"][:, sl], in_=bv[:])

    for ti, t0 in enumerate(range(0, T, TP)):
        prep_tile(ti, t0)

    TC = 512
    S = C.sb("S", [128, 64]); Y = C.sb("Y", [128, T]); junk = C.sb("junk", [128, 64]); sa = C.sb("sa", [128, 1])
    HSPL = [slice(0, 64), slice(64, 128)] if SPLIT else [slice(0, 128)]
    P.i('dve', 'memset', w=["S0", "S64"], ap=S[:], constant=0.0)
    types = ("W", "A", "B", "K", "R")
    ch = {n: [C.sb("ch_%s%d" % (n, i), [128, TC]) for i in range(2)] for n in types + ("V",)}
    E = {n: [C.sb("E_%s%d" % (n, i), [128, G, 64]) for i in range(2)] for n in types}
    BC = {n: [C.sb("BC_%s%d" % (n, i), [128, G * 64]) for i in range(2)] for n in types}
    gi = 0
    for ci, c0 in enumerate(range(0, T, TC)):
        cp = ci % 2
        for n in types + ("V",):
            P.d('sp', r=[SCR[n]], w=[ch[n][cp]], out=ch[n][cp][:], in_=SCR[n][:, c0:c0 + TC])
        for g0 in range(0, TC, G):
            gp = gi % 2; gi += 1
            for n in types:
                src = ch[n][cp]
                P.i('pool', 'tensor_tensor', r=[I2, src], w=[E[n][gp]], out=E[n][gp][:], in0=I2[:].unsqueeze(1).broadcast_to([128, G, 64]), in1=src[:, g0:g0 + G].unsqueeze(2).broadcast_to([128, G, 64]), op=ALU.mult)
                pb = C.bank()
                P.i('pe', 'matmul', r=[bo, E[n][gp]], w=[pb], out=pb[:, 0:G * 64], lhsT=bo[:], rhs=E[n][gp][:].rearrange("p a b -> p (a b)"), start=True, stop=True)
                P.i('act', 'activation', r=[pb], w=[BC[n][gp]], out=BC[n][gp][:], in_=pb[:, 0:G * 64], func=AF.Copy)
            vch = ch["V"][cp]
            for j in range(G):
                t = c0 + g0 + j
                js = slice(j * 64, (j + 1) * 64)
                for hs in HSPL:
                    Sk = "S%d" % hs.start
                    P.i('dve', 'scalar_tensor_tensor', r=[Sk, BC["A"][gp]], w=["junk" + Sk, "sa" + Sk], out=junk[hs, :], in0=S[hs, :], scalar=1.0, in1=BC["A"][gp][hs, js], op0=ALU.mult, op1=ALU.mult, accum_out=sa[hs, 0:1])
                for hs in HSPL:
                    Sk = "S%d" % hs.start
                    P.i('dve', 'tensor_tensor', r=[Sk, BC["W"][gp]], w=[Sk], out=S[hs, :], in0=S[hs, :], in1=BC["W"][gp][hs, js], op=ALU.mult)
                for hs in HSPL:
                    Sk = "S%d" % hs.start
                    P.i('dve', 'scalar_tensor_tensor', r=[Sk, BC["B"][gp], "sa" + Sk], w=[Sk], out=S[hs, :], in0=BC["B"][gp][hs, js], scalar=sa[hs, 0:1], in1=S[hs, :], op0=ALU.mult, op1=ALU.add)
                for hs in HSPL:
                    Sk = "S%d" % hs.start
                    P.i('dve', 'scalar_tensor_tensor', r=[Sk, BC["K"][gp], vch], w=[Sk], out=S[hs, :], in0=BC["K"][gp][hs, js], scalar=vch[hs, g0 + j:g0 + j + 1], in1=S[hs, :], op0=ALU.mult, op1=ALU.add)
                for hs in HSPL:
                    Sk = "S%d" % hs.start
                    P.i('dve', 'scalar_tensor_tensor', r=[Sk, BC["R"][gp]], w=["junk" + Sk, Y], out=junk[hs, :], in0=S[hs, :], scalar=1.0, in1=BC["R"][gp][hs, js], op0=ALU.mult, op1=ALU.mult, accum_out=Y[hs, t:t + 1])

    gt = [wdec, aa]; bvt = [gg, vv]
    yc = kk; ysq = kn; rs = tmp; oo = [at_, bt_]
    for ti, t0 in enumerate(range(0, T, TP)):
        par = ti % 2
        sl = slice(t0, t0 + TP)
        P.d('sp', r=[SCR["G"]], w=[gt[par]], out=gt[par][:], in_=SCR["G"][:, sl])
        P.d('sp', r=[SCR["BV"]], w=[bvt[par]], out=bvt[par][:], in_=SCR["BV"][:, sl])
        pb = C.bank()
        P.i('pe', 'matmul', r=[bo, Y], w=[pb], out=pb[:], lhsT=bo[:], rhs=Y[:, sl], start=True, stop=True)
        P.i('dve', 'scalar_tensor_tensor', r=[pb, Y], w=[yc], out=yc[:], in0=pb[:], scalar=-1.0 / 64, in1=Y[:, sl], op0=ALU.mult, op1=ALU.add)
        P.i('act', 'activation', r=[yc], w=[ysq], out=ysq[:], in_=yc[:], func=AF.Square)
        pb2 = C.bank()
        P.i('pe', 'matmul', r=[bo, ysq], w=[pb2], out=pb2[:], lhsT=bo[:], rhs=ysq[:], start=True, stop=True)
        P.i('dve', 'tensor_scalar', r=[pb2], w=[rs], out=rs[:], in0=pb2[:], scalar1=1.0 / 64, scalar2=LNX_EPS, op0=ALU.mult, op1=ALU.add)
        P.i('act', 'activation', r=[rs], w=[rs], out=rs[:], in_=rs[:], func=AF.Sqrt)
        P.i('dve', 'reciprocal', r=[rs], w=[rs], out=rs[:], in_=rs[:])
        P.i('dve', 'tensor_tensor', r=[yc, rs], w=[yc], out=yc[:], in0=yc[:], in1=rs[:], op=ALU.mult)
        P.i('dve', 'tensor_scalar', r=[yc, pv], w=[yc], out=yc[:], in0=yc[:], scalar1=col("ln_w"), scalar2=col("ln_b"), op0=ALU.mult, op1=ALU.add)
        P.i('dve', 'tensor_tensor', r=[yc, bvt[par]], w=[yc], out=yc[:], in0=yc[:], in1=bvt[par][:], op=ALU.add)
        o = oo[par]
        P.i('dve', 'tensor_tensor', r=[yc, gt[par]], w=[o], out=o[:], in0=yc[:], in1=gt[par][:], op=ALU.mult)
        P.d('pool', r=[o], out=oT[:, sl], in_=o[:], final=True)
    return C.finish()


NEGB = -30000.0
SCALE = 0.125
EPSN = 1e-6


def build_nsa(T, NBLK=None, DBG=False):
    C = Ctx(); P = C.P
    P.same_engine_sync = True
    nblk = T // 128
    NBLK = nblk if NBLK is None else NBLK
    NCMP = T // 16
    NCT = NCMP // 128
    NS = T // 64
    assert NS <= 128
    qT = C.dram_in("qT", [256, T]); kcT = C.dram_in("kcT", [64, T]); vcT = C.dram_in("vcT", [64, T])
    ksT = C.dram_in("ksT", [64, T]); kwT = C.dram_in("kwT", [64, T])
    vs = C.dram_in("vs", [T, 64]); vw = C.dram_in("vw", [T, 64]); gl = C.dram_in("gl", [T, 6])
    gain = C.dram_in("gain", [4, 64]); pe = C.dram_in("pe", [2, 32, 64]); w1 = C.dram_in("w1", [2, 32, 64, 64]); w2 = C.dram_in("w2", [2, 64, 64])
    out = C.dram_out("o", [T, 128])

    psS = [C.ps("psS%d" % i, [128, 512]) for i in range(3)]
    psSlc = C.ps("psSlc", [128, 512]); psWin = C.ps("psWin", [128, 512])
    psC = [C.ps("psC%d" % i, [128, 512]) for i in range(2)]
    psT = C.ps("psT", [128, 512])
    sidx = [0]
    def sbank():
        b = psS[sidx[0] % 3]; sidx[0] += 1
        return b

    ident = C.sb("ident", [128, 128])
    P.i('pool', 'memset', w=[ident], ap=ident[:], constant=1.0)
    P.i('pool', 'affine_select', r=[ident], w=[ident], out=ident[:], in_=ident[:], pattern=[[-1, 128]], compare_op=ALU.is_equal, fill=0.0, base=0, channel_multiplier=1)
    ones64 = C.sb("ones64", [64, 64])
    P.i('pool', 'memset', w=[ones64], ap=ones64[:], constant=1.0)
    Ef = C.sb("Ef", [128, 2048]); Ebig = C.sb("Ebig", [128, T], BF16)
    for k0 in range(0, T, 2048):
        P.i('pool', 'memset', w=[Ef], ap=Ef[:], constant=1.0)
        P.i('pool', 'affine_select', r=[Ef], w=[Ef], out=Ef[:], in_=Ef[:], pattern=[[1, 2048]], compare_op=ALU.is_ge, fill=0.0, base=k0, channel_multiplier=-64)
        P.i('pool', 'affine_select', r=[Ef], w=[Ef], out=Ef[:], in_=Ef[:], pattern=[[-1, 2048]], compare_op=ALU.is_ge, fill=0.0, base=63 - k0, channel_multiplier=64)
        P.i('pool', 'tensor_copy', r=[Ef], w=[Ebig], out=Ebig[:, k0:k0 + 2048], in_=Ef[:])
    Fbase = C.sb("Fbase", [128, 256])
    P.i('pool', 'memset', w=[Fbase], ap=Fbase[:], constant=0.0)
    P.i('pool', 'memset', w=[Fbase], ap=Fbase[0:64, 128:129], constant=2e4)
    P.i('pool', 'memset', w=[Fbase], ap=Fbase[0:64, 127:128], constant=4e4)
    P.i('pool', 'memset', w=[Fbase], ap=Fbase[64:128, 129:130], constant=2e4)
    P.i('pool', 'memset', w=[Fbase], ap=Fbase[64:128, 128:129], constant=4e4)
    gT = C.sb("gT", [64, 4]); P.d('sp', w=[gT], out=gT[:], in_=gain.rearrange("j d -> d j"), allow_slow_non_contiguous=True)
    peT = C.sb("peT", [64, 2, 32]); P.d('sp', w=[peT], out=peT[:], in_=pe.rearrange("a l d -> d a l"), allow_slow_non_contiguous=True)
    w1b = C.sb("w1b", [64, 2, 32, 64], BF16); w2b = C.sb("w2b", [64, 2, 64], BF16)
    big = C.sb("big", [128, T])
    for a in range(2):
        P.d('sp', w=[big], out=big[0:64, 0:2048].rearrange("d (l e) -> d l e", l=32), in_=w1[a].rearrange("l d e -> d l e"))
        P.i('dve', 'tensor_copy', r=[big], w=[w1b], out=w1b[:, a, :, :], in_=big[0:64, 0:2048].rearrange("d (l e) -> d l e", l=32))
    P.d('sp', w=[big], out=big[0:64, 0:128].rearrange("d (a f) -> d a f", a=2), in_=w2.rearrange("a e f -> e a f"))
    P.i('dve', 'tensor_copy', r=[big], w=[w2b], out=w2b[:], in_=big[0:64, 0:128].rearrange("d (a f) -> d a f", a=2))

    sqt = C.sb("sqt", [64, 512]); rst = C.sb("rst", [64, 512])
    def rms_fm(x_ap, W, gcol, out_ap, rkeys, wkeys):
        P.i('act', 'activation', r=rkeys, w=[sqt], out=sqt[:, 0:W], in_=x_ap, func=AF.Square)
        P.i('pe', 'matmul', r=[ones64, sqt], w=[psT], out=psT[0:64, 0:W], lhsT=ones64[:], rhs=sqt[:, 0:W], start=True, stop=True)
        P.i('dve', 'tensor_scalar', r=[psT], w=[rst], out=rst[:, 0:W], in0=psT[0:64, 0:W], scalar1=1.0 / 64, scalar2=EPSN, op0=ALU.mult, op1=ALU.add)
        P.i('act', 'activation', r=[rst], w=[rst], out=rst[:, 0:W], in_=rst[:, 0:W], func=AF.Sqrt)
        P.i('dve', 'reciprocal', r=[rst], w=[rst], out=rst[:, 0:W], in_=rst[:, 0:W])
        P.i('dve', 'scalar_tensor_tensor', r=rkeys + [rst, gT], w=wkeys, out=out_ap, in0=x_ap, scalar=gT[:, gcol:gcol + 1], in1=rst[:, 0:W], op0=ALU.mult, op1=ALU.mult)

    ksn = C.sb("ksn", [64, T], BF16); kwn = C.sb("kwn", [64, T], BF16)
    for (src, dst, gc) in ((ksT, ksn, 2), (kwT, kwn, 3)):
        P.d('sp', w=[big], out=big[0:64, :], in_=src[:, :])
        for t0 in range(0, T, 512):
            rms_fm(big[0:64, t0:t0 + 512], 512, gc, dst[:, t0:t0 + 512], [big], [dst])
    NT = T // 128
    vsa = C.sb("vsa", [128, NT, 65], BF16); vwa = C.sb("vwa", [128, NT, 65], BF16)
    for (src, dst) in ((vs, vsa), (vw, vwa)):
        P.d('sp', w=[big], out=big[:, 0:NT * 64].rearrange("p (n d) -> p n d", d=64), in_=src.rearrange("(n p) d -> p n d", p=128))
        P.i('dve', 'tensor_copy', r=[big], w=[dst], out=dst[:, :, 0:64], in_=big[:, 0:NT * 64].rearrange("p (n d) -> p n d", d=64))
        P.i('pool', 'memset', w=[dst], ap=dst[:, :, 64:65], constant=1.0)
    hidb = C.sb("hidb", [64, NCMP], BF16); kcmpT = C.sb("kcmpT", [64, NCMP], BF16)
    Vca = C.sb("Vca", [128, NCT, 193], BF16)
    tmpb = [C.sb("tmpb%d" % i, [64, 512], BF16) for i in range(2)]
    kcf = C.sb("kcf", [64, 512])
    ovf = C.sb("ovf", [128, 128])
    nvalid = NCMP - 1
    for a, src in ((0, kcT), (1, vcT)):
        P.d('sp', w=[big], out=big[0:64, :], in_=src[:, :])
        P.i('pool', 'memset', w=[hidb], ap=hidb[:], constant=0.0)
        for n0 in range(0, nvalid, 512):
            nw = min(512, nvalid - n0)
            for l in range(32):
                tb = tmpb[l % 2]
                st_ = 16 * n0 + l
                P.i('dve', 'tensor_scalar', r=[big, peT], w=[tb], out=tb[:, 0:nw], in0=big[0:64, st_:st_ + 16 * (nw - 1) + 1:16], scalar1=peT[:, a, l:l + 1], scalar2=None, op0=ALU.add)
                P.i('pe', 'matmul', r=[w1b, tb], w=[psT], out=psT[0:64, 0:nw], lhsT=w1b[:, a, l, :], rhs=tb[:, 0:nw], start=(l == 0), stop=(l == 31))
            P.i('act', 'activation', r=[psT], w=[hidb], out=hidb[:, n0:n0 + nw], in_=psT[0:64, 0:nw], func=AF.Silu)
        if a == 0:
            for n0 in range(0, NCMP, 512):
                nw = min(512, NCMP - n0)
                P.i('pe', 'matmul', r=[w2b, hidb], w=[psT], out=psT[0:64, 0:nw], lhsT=w2b[:, 0, :], rhs=hidb[:, n0:n0 + nw], start=True, stop=True)
                P.i('act', 'activation', r=[psT], w=[kcf], out=kcf[:, 0:nw], in_=psT[0:64, 0:nw], func=AF.Copy)
                rms_fm(kcf[:, 0:nw], nw, 1, kcmpT[:, n0:n0 + nw], [kcf], [kcmpT])
        else:
            for i in range(NCT):
                P.i('pe', 'matmul', r=[w2b, hidb], w=[psT], out=psT[:, 0:64], lhsT=hidb[:, i * 128:(i + 1) * 128], rhs=w2b[:, 1, :], start=True, stop=True)
                P.i('act', 'activation', r=[psT], w=[Vca], out=Vca[:, i, 0:64], in_=psT[:, 0:64], func=AF.Copy)
                P.i('pool', 'memset', w=[ovf], ap=ovf[:], constant=1.0)
                P.i('pool', 'affine_select', r=[ovf], w=[ovf], out=ovf[:], in_=ovf[:], pattern=[[-4, 128]], compare_op=ALU.is_ge, fill=0.0, base=128 * i + 1, channel_multiplier=1)
                P.i('pool', 'affine_select', r=[ovf], w=[ovf], out=ovf[:], in_=ovf[:], pattern=[[4, 128]], compare_op=ALU.is_ge, fill=0.0, base=3 - 128 * i, channel_multiplier=-1)
                P.i('pool', 'tensor_copy', r=[ovf], w=[Vca], out=Vca[:, i, 65:193], in_=ovf[:])
            P.i('pool', 'memset', w=[Vca], ap=Vca[:, :, 64:65], constant=1.0)
    gs = C.sb("gs", [128, nblk, 6])
    P.d('sp', w=[gs], out=gs[:], in_=gl.rearrange("(n p) j -> p n j", p=128))
    P.i('act', 'activation', r=[gs], w=[gs], out=gs[:], in_=gs[:], func=AF.Sigmoid)

    qf = [C.sb("qf%d" % i, [64, 4, 128]) for i in range(2)]
    qn = [C.sb("qn%d" % i, [64, 512], BF16) for i in range(2)]
    pT = [C.sb("pT%d" % i, [128, 512], BF16) for i in range(3)]
    pidx = [0]
    def pbuf():
        b = pT[pidx[0] % 3]; pidx[0] += 1
        return b
    cso = C.sb("cso", [128, 4, 193])
    rden = C.sb("rden", [128, 4]); imp = C.sb("imp", [128, 128]); imp2 = C.sb("imp2", [128, 128])
    m8 = C.sb("m8", [128, 8]); m8b = C.sb("m8b", [128, 8]); selb = C.sb("selb", [128, 128])
    selTs = [C.sb("selT%d" % i, [128, 2, 128], BF16) for i in range(2)]
    oacc = [C.sb("oacc%d" % i, [128, 128]) for i in range(2)]
    obr = C.sb("obr", [65, 256]); otr = C.sb("otr", [128, 2, 65]); coef = C.sb("coef", [128, 2]); dd = C.sb("dd", [128, 2])
    qv = qT.rearrange("(h d) t -> d h t", d=64)

    def stagesA(c):
        t0 = c * 128
        bp = c % 2
        q_f = qf[bp]; q_n = qn[bp]; oa = oacc[bp]; selT = selTs[bp]
        def A1():
            P.d('sp', w=[q_f], out=q_f[:], in_=qv[:, :, t0:t0 + 128])
            rms_fm(q_f[:].rearrange("d h t -> d (h t)"), 512, 0, q_n[:], [q_f], [q_n])

        def A2():
            nmax = 8 * c + 6
            tiles = [i for i in range(NCT) if 128 * i <= nmax]
            for ii, i in enumerate(tiles):
                sb_ = sbank()
                P.i('pe', 'matmul', r=[kcmpT, q_n], w=[sb_], out=sb_[:], lhsT=kcmpT[:, i * 128:(i + 1) * 128], rhs=q_n[:], start=True, stop=True)
                p_ = pbuf()
                P.i('act', 'activation', r=[sb_], w=[p_], out=p_[:], in_=sb_[:], func=AF.Exp, scale=SCALE)
                if 8 * c < 128 * i + 129:
                    P.i('pool', 'affine_select', r=[p_], w=[p_], out=p_[:].rearrange("p (h t) -> p h t", h=4), in_=p_[:].rearrange("p (h t) -> p h t", h=4), pattern=[[0, 4], [1, 128]], compare_op=ALU.is_ge, fill=0.0, base=t0 - 31 - 16 * 128 * i, channel_multiplier=-16)
                for h in range(4):
                    pc = psC[h // 2]
                    P.i('pe', 'matmul', r=[p_, Vca], w=[pc], out=pc[:, (h % 2) * 193:(h % 2) * 193 + 193], lhsT=p_[:, h * 128:(h + 1) * 128], rhs=Vca[:, i, :], start=(ii == 0 and h % 2 == 0), stop=(ii == len(tiles) - 1), skip_group_check=True)

        def A3():
            for hh in range(2):
                P.i('act', 'activation', r=[psC[hh]], w=[cso], out=cso[:, 2 * hh:2 * hh + 2, :], in_=psC[hh][:, 0:386].rearrange("p (h x) -> p h x", h=2), func=AF.Copy)
            P.i('dve', 'tensor_scalar', r=[cso], w=[rden], out=rden[:], in0=cso[:, :, 64], scalar1=1e-30, scalar2=None, op0=ALU.max)
            P.i('dve', 'reciprocal', r=[rden], w=[rden], out=rden[:], in_=rden[:])
            P.i('dve', 'scalar_tensor_tensor', r=[cso, rden, Fbase], w=[imp], out=imp[:], in0=cso[:, 0, 65:193], scalar=rden[:, 0:1], in1=Fbase[:, 128 - 2 * c:256 - 2 * c], op0=ALU.mult, op1=ALU.add)
            for h in range(1, 4):
                P.i('dve', 'scalar_tensor_tensor', r=[cso, rden, imp], w=[imp], out=imp[:], in0=cso[:, h, 65:193], scalar=rden[:, h:h + 1], in1=imp[:], op0=ALU.mult, op1=ALU.add)
            P.i('dve', 'tensor_scalar', r=[imp], w=[imp], out=imp[:, 0:1], in0=imp[:, 0:1], scalar1=1e4, scalar2=None, op0=ALU.add)
            P.i('dve', 'max', r=[imp], w=[m8], out=m8[:], in_=imp[:])
            P.i('dve', 'match_replace', r=[imp, m8], w=[imp2], out=imp2[:], in_to_replace=m8[:], in_values=imp[:], imm_value=-1e9)
            P.i('dve', 'max', r=[imp2], w=[m8b], out=m8b[:], in_=imp2[:])
            P.i('dve', 'tensor_scalar', r=[imp, m8b], w=[selb], out=selb[:], in0=imp[:], scalar1=m8b[:, 7:8], scalar2=NEGB, op0=ALU.is_lt, op1=ALU.mult)
            P.i('pe', 'transpose', r=[selb, ident], w=[psT], out=psT[:, 0:128], in_=selb[:], identity=ident[:])
            P.i('dve', 'tensor_copy', r=[psT], w=[selT], out=selT[:], in_=psT[:, 0:128].unsqueeze(1).broadcast_to([128, 2, 128]))
            P.i('dve', 'tensor_tensor', r=[gs, rden], w=[coef], out=coef[:], in0=gs[:, c, 0:6:3], in1=rden[:, 0:2], op=ALU.mult)
            for h in range(2):
                P.i('dve', 'tensor_scalar', r=[cso, coef], w=[oa], out=oa[:, h * 64:(h + 1) * 64], in0=cso[:, h, 0:64], scalar1=coef[:, h:h + 1], scalar2=None, op0=ALU.mult)

        return [A1, A2, A3]

    def blockB(c, inter):
        t0 = c * 128
        bp = c % 2
        q_f = qf[bp]; q_n = qn[bp]; oa = oacc[bp]; selT = selTs[bp]
        n_t = c + 1
        for j in range(c + 1):
            sb_ = sbank()
            P.i('pe', 'matmul', r=[ksn, q_n], w=[sb_], out=sb_[:, 0:256], lhsT=ksn[:, j * 128:(j + 1) * 128], rhs=q_n[:, 0:256], start=True, stop=False)
            P.i('pe', 'matmul', r=[Ebig, selT], w=[sb_], out=sb_[:, 0:256], lhsT=Ebig[:, j * 128:(j + 1) * 128], rhs=selT[:].rearrange("p h t -> p (h t)"), start=False, stop=True)
            p_ = pbuf()
            P.i('act', 'activation', r=[sb_], w=[p_], out=p_[:, 0:256], in_=sb_[:, 0:256], func=AF.Exp, scale=SCALE)
            if j == c:
                P.i('pool', 'affine_select', r=[p_], w=[p_], out=p_[:, 0:256].rearrange("p (h t) -> p h t", h=2), in_=p_[:, 0:256].rearrange("p (h t) -> p h t", h=2), pattern=[[0, 2], [1, 128]], compare_op=ALU.is_ge, fill=0.0, base=0, channel_multiplier=-1)
            P.i('pe', 'matmul', r=[vsa, p_], w=[psSlc], out=psSlc[0:65, 0:256], lhsT=vsa[:, j, :], rhs=p_[:, 0:256], start=(j == 0), stop=(j == c))
            while inter and (j + 1) * 4 >= n_t * (4 - len(inter)):
                inter.pop(0)()
        while inter:
            inter.pop(0)()
        j0 = max(0, c - 4)
        for j in range(j0, c + 1):
            sb_ = sbank()
            P.i('pe', 'matmul', r=[kwn, q_n], w=[sb_], out=sb_[:, 0:256], lhsT=kwn[:, j * 128:(j + 1) * 128], rhs=q_n[:, 0:256], start=True, stop=True)
            p_ = pbuf()
            P.i('act', 'activation', r=[sb_], w=[p_], out=p_[:, 0:256], in_=sb_[:, 0:256], func=AF.Exp, scale=SCALE)
            if j == c:
                P.i('pool', 'affine_select', r=[p_], w=[p_], out=p_[:, 0:256].rearrange("p (h t) -> p h t", h=2), in_=p_[:, 0:256].rearrange("p (h t) -> p h t", h=2), pattern=[[0, 2], [1, 128]], compare_op=ALU.is_ge, fill=0.0, base=0, channel_multiplier=-1)
            if j == c - 4:
                P.i('pool', 'affine_select', r=[p_], w=[p_], out=p_[:, 0:256].rearrange("p (h t) -> p h t", h=2), in_=p_[:, 0:256].rearrange("p (h t) -> p h t", h=2), pattern=[[0, 2], [-1, 128]], compare_op=ALU.is_ge, fill=0.0, base=-1, channel_multiplier=1)
            P.i('pe', 'matmul', r=[vwa, p_], w=[psWin], out=psWin[0:65, 0:256], lhsT=vwa[:, j, :], rhs=p_[:, 0:256], start=(j == j0), stop=(j == c))
        for br, pacc in ((1, psSlc), (2, psWin)):
            P.i('act', 'activation', r=[pacc], w=[obr], out=obr[:], in_=pacc[0:65, 0:256], func=AF.Copy)
            for h in range(2):
                P.i('pe', 'transpose', r=[obr, ident], w=[psT], out=psT[:, h * 65:h * 65 + 65], in_=obr[:, h * 128:(h + 1) * 128], identity=ident[0:65, 0:65])
            P.i('act', 'activation', r=[psT], w=[otr], out=otr[:], in_=psT[:, 0:130].rearrange("p (h x) -> p h x", h=2), func=AF.Copy)
            P.i('dve', 'tensor_scalar', r=[otr], w=[dd], out=dd[:], in0=otr[:, :, 64], scalar1=1e-30, scalar2=None, op0=ALU.max)
            P.i('dve', 'reciprocal', r=[dd], w=[dd], out=dd[:], in_=dd[:])
            P.i('dve', 'tensor_tensor', r=[gs, dd], w=[coef], out=coef[:], in0=gs[:, c, br:6:3], in1=dd[:], op=ALU.mult)
            for h in range(2):
                P.i('dve', 'scalar_tensor_tensor', r=[otr, coef, oa], w=[oa], out=oa[:, h * 64:(h + 1) * 64], in0=otr[:, h, 0:64], scalar=coef[:, h:h + 1], in1=oa[:, h * 64:(h + 1) * 64], op0=ALU.mult, op1=ALU.add)
        P.d('pool', r=[oa], out=out[t0:t0 + 128, :], in_=oa[:], final=True)
        if DBG and c == 1:
            d1 = C.dram_out("d_obr", [65, 256]); P.d('sp', r=[obr], out=d1[:, :], in_=obr[:], final=True)
            d2 = C.dram_out("d_otr", [128, 130]); P.d('sp', r=[otr], out=d2[:, :], in_=otr[:].rearrange("p h x -> p (h x)"), final=True)
            d3 = C.dram_out("d_dd", [128, 2]); P.d('sp', r=[dd], out=d3[:, :], in_=dd[:], final=True)
            d4 = C.dram_out("d_coef", [128, 2]); P.d('sp', r=[coef], out=d4[:, :], in_=coef[:], final=True)
            d5 = C.dram_out("d_vwa", [128, 65], BF16); P.d('sp', r=[vwa], out=d5[:, :], in_=vwa[:, 1, :], final=True)
            d6 = C.dram_out("d_kwn", [64, 256], BF16); P.d('sp', r=[kwn], out=d6[:, :], in_=kwn[:, 0:256], final=True)
            d7 = C.dram_out("d_qn", [64, 512], BF16); P.d('sp', r=[q_n], out=d7[:, :], in_=q_n[:], final=True)
            d8 = C.dram_out("d_gs", [128, 6]); P.d('sp', r=[gs], out=d8[:, :], in_=gs[:, c, :], final=True)

    for st_ in stagesA(0):
        st_()
    for c in range(NBLK):
        blockB(c, stagesA(c + 1) if c + 1 < NBLK else [])
    return C.finish()


_CACHE = {}

def _get(name, fn):
    if name not in _CACHE:
        _CACHE[name] = fn()
    return _CACHE[name]


def _run(nc, in_maps):
    in_maps = [{k: np.ascontiguousarray(v, dtype=np.float32) for k, v in m.items()} for m in in_maps]
    res = run_bass_kernel_spmd(nc, in_maps, core_ids=list(range(8)))
    return res.results


def kernel(x, norm_mix, norm_ffn, w_in, qk_gain, cmp_pe, cmp_w1, cmp_w2, rwkv_mu, rwkv_w0,
           rwkv_w_up, rwkv_a0, rwkv_a_up, rwkv_g_up, rwkv_k_k, rwkv_k_a, rwkv_r_k, rwkv_ln_w,
           rwkv_ln_b, vres_v0, vres_v1, vres_v2, proj_nsa, proj_rwkv, w_out, ffn_up, ffn_conv, ffn_down):
    f = lambda a: np.asarray(a, dtype=np.float32)
    x = f(x)
    B, S, Dm = x.shape
    NTOK = B * S
    TS = NTOK // 8
    CPB = 8 // B
    xT = np.ascontiguousarray(x.reshape(NTOK, Dm).T)
    NMIX = 3096
    vfirst = [None] * 8
    for l in range(4):
        nc = _get("proj", lambda: build_proj(TS, NMIX))
        Wm = f(w_in[l])[:, :NMIX]
        res = _run(nc, [{"xT": xT[:, c * TS:(c + 1) * TS], "gain": f(norm_mix[l]), "W": Wm} for c in range(8)])
        uT = np.concatenate([r["uT"] for r in res], axis=1)
        layer0 = (l == 0)
        nc = _get("rwkv%d" % int(layer0), lambda: build_rwkv(S, layer0))
        mu = f(rwkv_mu[l])
        maps = []
        for c in range(8):
            b = c // CPB; hp = c % CPB
            ub = uT[:, b * S:(b + 1) * S]
            rw = ub[1304:3096]
            own = slice(hp * 128, (hp + 1) * 128)
            pvec = np.zeros((128, NPV), np.float32)
            pvec[:, PV["mu_r"]] = mu[0:512][own]; pvec[:, PV["mu_k"]] = mu[512:1024][own]; pvec[:, PV["mu_v"]] = mu[1024:1536][own]
            pvec[:, PV["w0"]] = f(rwkv_w0[l])[own]; pvec[:, PV["a0"]] = f(rwkv_a0[l])[own]; pvec[:, PV["k_k"]] = f(rwkv_k_k[l])[own]
            pvec[:, PV["k_a"]] = f(rwkv_k_a[l])[own]; pvec[:, PV["r_k"]] = f(rwkv_r_k[l]).reshape(512)[own]
            pvec[:, PV["ln_w"]] = f(rwkv_ln_w[l])[own]; pvec[:, PV["ln_b"]] = f(rwkv_ln_b[l])[own]
            if not layer0:
                pvec[:, PV["v0"]] = f(vres_v0[l - 1])[own]
            pvec[:, PV["mu_g"]] = mu[1664:1792]; pvec[0:64, PV["mu_w"]] = mu[1536:1600]; pvec[0:64, PV["mu_a"]] = mu[1600:1664]
            for cc in range(4):
                pvec[:, PV["mu_va"] + cc] = mu[1024 + cc * 128:1024 + (cc + 1) * 128]
            m = {"zr": rw[0:512][own], "zk": rw[512:1024][own], "zvo": rw[1024:1536][own], "zw": rw[1536:1600], "za": rw[1600:1664], "zg": rw[1664:1792],
                 "pvec": pvec, "w_up": f(rwkv_w_up[l])[:, own], "a_up": f(rwkv_a_up[l])[:, own], "g_up": f(rwkv_g_up[l])[:, own]}
            if not layer0:
                m.update({"zva": rw[1024:1536], "v1": f(vres_v1[l - 1]), "v2": f(vres_v2[l - 1])[:, own], "vfirst": vfirst[c]})
            maps.append(m)
        res = _run(nc, maps)
        obT = np.zeros((512, NTOK), np.float32)
        for c in range(8):
            b = c // CPB; hp = c % CPB
            obT[hp * 128:(hp + 1) * 128, b * S:(b + 1) * S] = res[c]["oT"]
            if layer0:
                vfirst[c] = res[c]["vfo"]
        nc = _get("nsa", lambda: build_nsa(S))
        maps = []
        heads_of = []
        for c in range(8):
            b = c // CPB; g = (c % CPB) // 2; hh = c % 2
            ub = uT[:, b * S:(b + 1) * S]
            heads = [g * 4 + 2 * hh, g * 4 + 2 * hh + 1, g * 4 + 2 * (1 - hh), g * 4 + 2 * (1 - hh) + 1]
            heads_of.append(heads)
            gsl = slice(g * 64, (g + 1) * 64)
            glr = ub[1280:1304]
            m = {"qT": np.concatenate([ub[h * 64:(h + 1) * 64] for h in heads], 0),
                 "kcT": ub[512:640][gsl], "vcT": ub[640:768][gsl], "ksT": ub[768:896][gsl], "kwT": ub[1024:1152][gsl],
                 "vs": ub[896:1024][gsl].T, "vw": ub[1152:1280][gsl].T,
                 "gl": np.concatenate([glr[h * 3:(h + 1) * 3] for h in heads[:2]], 0).T,
                 "gain": f(qk_gain[l]), "pe": f(cmp_pe[l]), "w1": f(cmp_w1[l]), "w2": f(cmp_w2[l])}
            maps.append(m)
        res = _run(nc, maps)
        oaT = np.zeros((512, NTOK), np.float32)
        for c in range(8):
            b = c // CPB
            h0 = heads_of[c][0]
            oaT[h0 * 64:h0 * 64 + 128, b * S:(b + 1) * S] = res[c]["o"].T
        nc = _get("merge", lambda: build_merge(TS))
        Wg = f(w_in[l])[:, NMIX:]
        res = _run(nc, [{"xT": xT[:, c * TS:(c + 1) * TS], "gain": f(norm_mix[l]), "Wg": Wg, "oaT": oaT[:, c * TS:(c + 1) * TS], "obT": obT[:, c * TS:(c + 1) * TS],
                         "PA": f(proj_nsa[l]), "PB": f(proj_rwkv[l]), "Wo": f(w_out[l])} for c in range(8)])
        x1T = np.concatenate([r["x1T"] for r in res], axis=1)
        nc = _get("ffn", lambda: build_ffn(TS))
        maps = []
        for c in range(8):
            t0 = c * TS
            xin = np.zeros((Dm, HW + TS), np.float32)
            xin[:, HW:] = x1T[:, t0:t0 + TS]
            if t0 % S != 0:
                xin[:, :HW] = x1T[:, t0 - HW:t0]
            maps.append({"xT": xin, "gain": f(norm_ffn[l]), "Wup": f(ffn_up[l]), "cw": f(ffn_conv[l]), "Wd": f(ffn_down[l])})
        res = _run(nc, maps)
        xT = np.concatenate([r["x2T"] for r in res], axis=1)
    return np.ascontiguousarray(xT.T).reshape(B, S, Dm).astype(np.float32)
```
